# Optimizing a Trainium2 kernel written in Bass

```python
import math
import jax
import jax.numpy as jnp
from jax import lax
import numpy as np

D_MODEL = 1024
BATCH = 32
SEQ = 2048
DEPTH = 2
DEC_BATCH = 8
DEC_SEQ = 2048
PAST_LEN = 128

N_META = 16
GRID_W = 64
Q_BLOCK = 128
MIX_WIDTH = D_MODEL
A_QK_DIM = 64
A_V_DIM = 2 * A_QK_DIM
A_HEADS = (MIX_WIDTH // 2) // A_V_DIM
A_SCALE = 1.0 / math.sqrt(A_QK_DIM)
B_HEAD_DIM = 64
B_HEADS = (MIX_WIDTH // 2) // B_HEAD_DIM
B_KV_HEADS = 2
B_GROUP = B_HEADS // B_KV_HEADS
B_SCALE = 1.0 / math.sqrt(B_HEAD_DIM)
AXIS_DIM = B_HEAD_DIM // 2
ROPE_THETA = 10000.0
D_FF = 4 * D_MODEL
N_BUCKETS = 32
MAX_DISTANCE = 128
EPS = 1e-6
A_Q_W = A_HEADS * 2 * A_QK_DIM
A_K_W = A_HEADS * 2 * A_QK_DIM
A_V_W = A_HEADS * A_V_DIM
B_Q_W = B_HEADS * B_HEAD_DIM
B_KV_W = B_KV_HEADS * B_HEAD_DIM
IN_WIDTH = A_Q_W + A_K_W + A_V_W + B_Q_W + 2 * B_KV_W
OUT_WIDTH = A_V_W + B_Q_W

kernel_name = "hymba_diffattn_axialgqa_encoder"


def _rms(x, g):
    xf = x.astype(jnp.float32)
    y = xf * lax.rsqrt(jnp.mean(xf * xf, axis=-1, keepdims=True) + EPS)
    return (y * g.astype(jnp.float32)).astype(x.dtype)


def _t5_bucket(rel):
    nb = N_BUCKETS // 2
    max_exact = nb // 2
    base = jnp.where(rel > 0, nb, 0)
    n = jnp.abs(rel)
    n_f = jnp.maximum(n, 1).astype(jnp.float32)
    large = max_exact + (jnp.log(n_f / max_exact) / math.log(MAX_DISTANCE / max_exact)
                         * (nb - max_exact)).astype(jnp.int32)
    large = jnp.minimum(large, nb - 1)
    return base + jnp.where(n < max_exact, n, large)


def _rope(x, cos, sin):
    xp = x.astype(jnp.float32).reshape(x.shape[:-1] + (B_HEAD_DIM // 2, 2))
    x0, x1 = xp[..., 0], xp[..., 1]
    c = cos[None, :, None, :]
    s = sin[None, :, None, :]
    out = jnp.stack([x0 * c - x1 * s, x0 * s + x1 * c], axis=-1)
    return out.reshape(x.shape).astype(x.dtype)


def _sweep(block_fn, q):
    b = q.shape[0]
    s = q.shape[1] - N_META
    nblk = s // Q_BLOCK
    out_meta = block_fn(q[:, :N_META], jnp.int32(0))
    qr = q[:, N_META:].reshape((b, nblk, Q_BLOCK) + q.shape[2:])
    qr = jnp.moveaxis(qr, 1, 0)
    starts = N_META + Q_BLOCK * jnp.arange(nblk, dtype=jnp.int32)
    out_r = lax.map(lambda a: block_fn(a[0], a[1]), (qr, starts))
    out_r = jnp.moveaxis(out_r, 0, 1).reshape((b, s) + out_r.shape[3:])
    return jnp.concatenate([out_meta, out_r], axis=1)


def _layer(x, layer_idx, bias_vec, cos, sin, g_attn_pre, w_in, lq1, lk1, lq2, lk2, g_subln,
           g_qnorm, g_knorm, w_out, g_attn_post, g_mlp_pre, w_up, w_down, g_mlp_post):
    b, L, _ = x.shape
    h = _rms(x, g_attn_pre)
    proj = h @ w_in
    cuts = np.cumsum([A_Q_W, A_K_W, A_V_W, B_Q_W, B_KV_W]).tolist()
    a_q, a_k, a_v, b_q, b_k, b_v = jnp.split(proj, cuts, axis=-1)

    a_q = a_q.reshape(b, L, A_HEADS, 2, A_QK_DIM)
    a_k = a_k.reshape(b, L, A_HEADS, 2, A_QK_DIM)
    a_v = a_v.reshape(b, L, A_HEADS, A_V_DIM)
    lam_init = 0.8 - 0.6 * math.exp(-0.3 * layer_idx)
    lam = (jnp.exp(jnp.sum(lq1.astype(jnp.float32) * lk1.astype(jnp.float32)))
           - jnp.exp(jnp.sum(lq2.astype(jnp.float32) * lk2.astype(jnp.float32))) + lam_init)
    kpos = jnp.arange(L, dtype=jnp.int32)

    def diff_block(qb, start):
        qpos = start + jnp.arange(qb.shape[1], dtype=jnp.int32)
        idx = kpos[None, :] - qpos[:, None] + (L - 1)
        bias = bias_vec[:, idx]
        logits = (jnp.einsum('bqhmd,bkhmd->bhmqk', qb, a_k).astype(jnp.float32) * A_SCALE
                  + bias[None, :, None])
        p = jax.nn.softmax(logits, axis=-1)
        w = (p[:, :, 0] - lam * p[:, :, 1]).astype(a_v.dtype)
        return jnp.einsum('bhqk,bkhe->bqhe', w, a_v)

    o_a = _sweep(diff_block, a_q)
    o_a = (_rms(o_a, g_subln) * (1.0 - lam_init)).reshape(b, L, A_V_W)

    b_q = _rope(_rms(b_q.reshape(b, L, B_HEADS, B_HEAD_DIM), g_qnorm), cos, sin)
    b_k = _rope(_rms(b_k.reshape(b, L, B_KV_HEADS, B_HEAD_DIM), g_knorm), cos, sin)
    b_v = b_v.reshape(b, L, B_KV_HEADS, B_HEAD_DIM)
    b_q = b_q.reshape(b, L, B_KV_HEADS, B_GROUP, B_HEAD_DIM)

    def gqa_block(qb, _start):
        logits = jnp.einsum('bqgrd,bkgd->bgrqk', qb, b_k).astype(jnp.float32) * B_SCALE
        p = jax.nn.softmax(logits, axis=-1).astype(b_v.dtype)
        return jnp.einsum('bgrqk,bkgd->bqgrd', p, b_v)

    o_b = _sweep(gqa_block, b_q).reshape(b, L, B_Q_W)

    mixed = jnp.concatenate([o_a, o_b], axis=-1) @ w_out
    x = x + _rms(mixed, g_attn_post)

    h = _rms(x, g_mlp_pre)
    u = jnp.square(jax.nn.relu(h @ w_up))
    x = x + _rms(u @ w_down, g_mlp_post)
    return x


def _forward(x, meta_tokens, rel_bias, g_attn_pre, w_in, lambda_q1, lambda_k1, lambda_q2, lambda_k2,
             g_subln, g_qnorm, g_knorm, w_out, g_attn_post, g_mlp_pre, w_up, w_down, g_mlp_post):
    b, s, _ = x.shape
    rows = s // GRID_W
    L = N_META + s
    meta = jnp.broadcast_to(meta_tokens.astype(x.dtype)[None], (b, N_META, D_MODEL))
    h = jnp.concatenate([meta, x], axis=1)

    rel = jnp.arange(2 * L - 1, dtype=jnp.int32) - (L - 1)
    bias_vec = rel_bias.astype(jnp.float32)[_t5_bucket(rel)].T

    t = jnp.arange(rows * GRID_W, dtype=jnp.int32)
    row = (t // GRID_W).astype(jnp.float32)
    col = (t % GRID_W).astype(jnp.float32)
    inv_freq = ROPE_THETA ** (-jnp.arange(0, AXIS_DIM, 2, dtype=jnp.float32) / AXIS_DIM)
    ang_real = jnp.concatenate([row[:, None] * inv_freq, col[:, None] * inv_freq], axis=-1)
    ang = jnp.concatenate([jnp.zeros((N_META, B_HEAD_DIM // 2), jnp.float32), ang_real], axis=0)
    cos, sin = jnp.cos(ang), jnp.sin(ang)

    for l in range(DEPTH):
        h = _layer(h, l, bias_vec, cos, sin, g_attn_pre[l], w_in[l], lambda_q1[l], lambda_k1[l],
                   lambda_q2[l], lambda_k2[l], g_subln[l], g_qnorm[l], g_knorm[l], w_out[l],
                   g_attn_post[l], g_mlp_pre[l], w_up[l], w_down[l], g_mlp_post[l])
    return h[:, N_META:]


def setup_inputs(seed: int = 0) -> dict:
    key = jax.random.key(seed)
    ks = jax.random.split(key, 20)
    f32 = jnp.float32

    def gain(k, n):
        return 1.0 + 0.05 * jax.random.normal(k, (DEPTH, n), f32)

    return {
        "x_prompt": jax.random.normal(ks[0], (BATCH, SEQ, D_MODEL), f32),
        "x_sample": jax.random.normal(ks[1], (DEC_BATCH, DEC_SEQ, D_MODEL), f32),
        "meta_tokens": jax.random.normal(ks[2], (N_META, D_MODEL), f32),
        "rel_bias": 0.5 * jax.random.normal(ks[3], (N_BUCKETS, A_HEADS), f32),
        "g_attn_pre": gain(ks[4], D_MODEL),
        "w_in": jax.random.normal(ks[5], (DEPTH, D_MODEL, IN_WIDTH), f32) * D_MODEL ** -0.5,
        "lambda_q1": 0.1 * jax.random.normal(ks[6], (DEPTH, A_QK_DIM), f32),
        "lambda_k1": 0.1 * jax.random.normal(ks[7], (DEPTH, A_QK_DIM), f32),
        "lambda_q2": 0.1 * jax.random.normal(ks[8], (DEPTH, A_QK_DIM), f32),
        "lambda_k2": 0.1 * jax.random.normal(ks[9], (DEPTH, A_QK_DIM), f32),
        "g_subln": gain(ks[10], A_V_DIM),
        "g_qnorm": gain(ks[11], B_HEAD_DIM),
        "g_knorm": gain(ks[12], B_HEAD_DIM),
        "w_out": jax.random.normal(ks[13], (DEPTH, OUT_WIDTH, D_MODEL), f32) * OUT_WIDTH ** -0.5,
        "g_attn_post": gain(ks[14], D_MODEL),
        "g_mlp_pre": gain(ks[15], D_MODEL),
        "w_up": jax.random.normal(ks[16], (DEPTH, D_MODEL, D_FF), f32) * D_MODEL ** -0.5,
        "w_down": jax.random.normal(ks[17], (DEPTH, D_FF, D_MODEL), f32) * D_FF ** -0.5,
        "g_mlp_post": gain(ks[18], D_MODEL),
    }


def reference(x_prompt, x_sample, meta_tokens, rel_bias, g_attn_pre, w_in, lambda_q1, lambda_k1,
              lambda_q2, lambda_k2, g_subln, g_qnorm, g_knorm, w_out, g_attn_post, g_mlp_pre,
              w_up, w_down, g_mlp_post):
    y_prompt = _forward(x_prompt, meta_tokens, rel_bias, g_attn_pre, w_in, lambda_q1, lambda_k1,
                        lambda_q2, lambda_k2, g_subln, g_qnorm, g_knorm, w_out, g_attn_post,
                        g_mlp_pre, w_up, w_down, g_mlp_post)
    y_sample = _forward(x_sample, meta_tokens, rel_bias, g_attn_pre, w_in, lambda_q1, lambda_k1,
                        lambda_q2, lambda_k2, g_subln, g_qnorm, g_knorm, w_out, g_attn_post,
                        g_mlp_pre, w_up, w_down, g_mlp_post)
    return (y_prompt, y_sample)
```

```python
import math
from contextlib import ExitStack
import numpy as np
import concourse.bass as bass
import concourse.mybir as mybir
from concourse.bass_utils import run_bass_kernel_spmd

F32 = mybir.dt.float32
BF16 = mybir.dt.bfloat16
AF = mybir.ActivationFunctionType
ALU = mybir.AluOpType
AX = mybir.AxisListType

DM = 1024
SEQ = 2048
NMETA = 16
L = SEQ + NMETA
DEPTH = 2
NCH = 8
EPS = 1e-6
BLOCKS = [(0, 512), (512, 512), (1024, 512), (1536, 512), (2048, 16)]
KTILES = [(t * 128, 128) for t in range(16)] + [(2048, 16)]
UPL = 92
U_AQ, U_AK, U_BQ, U_BQS, U_BK, U_BKS, U_WO, U_UP, U_DN = 0, 4, 8, 12, 16, 18, 20, 28, 60
BVR_LO = 1024
HKW = 1152
PG = 256


def pos0(col):
    return NMETA + col if col < SEQ else col - SEQ


def t5_bucket(rel):
    nb, me = 16, 8
    base = np.where(rel > 0, nb, 0)
    n = np.abs(rel)
    nf = np.maximum(n, 1).astype(np.float32)
    large = me + (np.log(nf / np.float32(me)) / np.float32(math.log(128 / 8)) * np.float32(nb - me)).astype(np.int32)
    large = np.minimum(large, nb - 1)
    return base + np.where(n < me, n, large)


QBLK = list(BLOCKS)


def q_segments(c0, n):
    if c0 + n <= SEQ or c0 >= SEQ:
        return [(0, n)]
    return [(0, SEQ - c0), (SEQ - c0, c0 + n - SEQ)]


def tile_bias_spec(kcol, nk, qcol, nq):
    D = pos0(kcol) - pos0(qcol)
    rel = D + np.arange(nk)[:, None] - np.arange(nq)[None, :]
    b = t5_bucket(rel)
    if (b == b.flat[0]).all():
        return ("const", int(b.flat[0]))
    return ("band", L - nk - D)


class Op:
    __slots__ = ("eng", "fn", "waits", "signal", "key", "seq", "clock", "clock2", "is_dma", "sigval", "idx")


class Prog:
    ENG = ("pe", "act", "dve", "pool", "sp")
    DMAK = {"sp": 8, "act": 4, "pool": 2}

    def __init__(self):
        self.streams = {e: [] for e in self.ENG}
        self.seen = {e: {} for e in self.ENG}
        self.lastw = {}
        self.readers = {}
        self.epoch = 0
        self.cnt = {}
        self.dma_n = {q: 0 for q in self.DMAK}
        self.dma_hist = {q: [] for q in self.DMAK}
        self.done = {e: {} for e in self.ENG}
        self.nops = 0

    def new_epoch(self):
        for e in self.ENG:
            k = (e, self.epoch)
            if k in self.cnt:
                self.done[e] = dict(self.done[e])
                self.done[e][k] = self.cnt[k] - 1
        self.epoch += 1

    def add(self, eng, fn, reads=(), writes=(), dma=False):
        op = Op()
        op.eng, op.fn, op.is_dma, op.signal, op.sigval = eng, fn, dma, dma, 0
        op.idx = self.nops
        self.nops += 1
        deps = {}
        lastw, readers = self.lastw, self.readers
        if eng != "pe" and reads:
            pr = [r for r in reads if r[0] == "ps"]
            if pr:
                writes = list(writes) + pr
                reads = [r for r in reads if r[0] != "ps"]
        for r in reads:
            w = lastw.get(r)
            if w is not None:
                deps[w.idx] = w
        for r in writes:
            w = lastw.get(r)
            if w is not None:
                deps[w.idx] = w
            rs = readers.get(r)
            if rs:
                for o in rs:
                    deps[o.idx] = o
        if dma:
            K = self.DMAK[eng]
            i = self.dma_n[eng]
            if i >= K:
                o = self.dma_hist[eng][i - K]
                deps[o.idx] = o
        elif eng == "pool" and DBG.get("poolser", 0):
            for o in reversed(self.streams[eng]):
                if not o.is_dma:
                    deps[o.idx] = o
                    break
        seen = self.seen[eng]
        waits = []
        cow = False
        for di in sorted(deps):
            d = deps[di]
            if (not d.is_dma) and d.eng == eng and eng == "pe":
                continue
            if seen.get(d.key, -1) >= d.seq:
                continue
            waits.append(d)
            d.signal = True
            if not cow:
                seen = dict(seen)
                self.seen[eng] = seen
                cow = True
            seen[d.key] = d.seq
            for kk, ss in d.clock.items():
                if seen.get(kk, -1) < ss:
                    seen[kk] = ss
            for kk, ss in d.clock2.items():
                if seen.get(kk, -1) < ss:
                    seen[kk] = ss
        op.waits = waits
        if dma:
            K = self.DMAK[eng]
            i = self.dma_n[eng]
            self.dma_n[eng] = i + 1
            op.key = ("dma", eng, i % K)
            op.seq = i // K + 1
            self.dma_hist[eng].append(op)
            op.clock = seen
            op.clock2 = {}
        else:
            op.key = (eng, self.epoch)
            op.seq = self.cnt.get(op.key, 0)
            self.cnt[op.key] = op.seq + 1
            op.clock = seen
            op.clock2 = self.done[eng]
        for r in reads:
            readers.setdefault(r, []).append(op)
        for r in writes:
            lastw[r] = op
            readers[r] = []
        self.streams[eng].append(op)
        return op

    def finalize(self):
        keys = set()
        for e in self.ENG:
            c = {}
            for op in self.streams[e]:
                if op.is_dma:
                    op.sigval = 16 * op.seq
                    keys.add(op.key)
                elif op.signal:
                    c[op.key] = c.get(op.key, 0) + 1
                    op.sigval = c[op.key]
                    keys.add(op.key)
        return sorted(keys, key=str)


class Buf:
    def __init__(self, name, tens, F_el, base_dt, boff, dt, ncols):
        self.name, self.tens, self.boff, self.dt, self.ncols = name, tens, boff, dt, ncols
        self.esz = 4 if dt == F32 else 2
        bsz = 4 if base_dt == F32 else 2
        lo = boff // bsz
        hi = lo + ncols * self.esz // bsz
        v = tens[:, lo:hi]
        if dt != base_dt:
            v = v.bitcast(dt)
        self.v = v

    def pages(self, c0, n):
        b0 = self.boff + c0 * self.esz
        b1 = self.boff + (c0 + n) * self.esz
        return [(self.name, p) for p in range(b0 // PG, (b1 - 1) // PG + 1)]

    def ap(self, c0, n, p0=0, p1=128):
        return self.v[p0:p1, c0:c0 + n]

    def a3(self, c0, dims, p0=0, p1=128):
        base = self.v[p0:p1, c0:c0 + 1]
        pp = base.ap[0]
        return bass.AP(self.v.tensor, base.offset, [list(pp)] + [list(d) for d in dims])


DBG = {}


def build_nc(nseq, depth=DEPTH, stop_after=None):
    kvp = DBG.get("kv", "pkrv")
    nc = bass.Bass("TRN2", target_bir_lowering=False)
    NU = depth * UPL
    x_d = nc.dram_tensor("x", [nseq, SEQ, DM], F32, kind="ExternalInput")
    meta_d = nc.dram_tensor("meta", [NMETA, DM], F32, kind="ExternalInput")
    wu_d = nc.dram_tensor("wu", [NU, 128, 1024], F32, kind="ExternalInput")
    wv_d = nc.dram_tensor("wv", [depth, 128, 5120], F32, kind="ExternalInput")
    gains_d = nc.dram_tensor("gains", [depth, 128, 32], F32, kind="ExternalInput")
    sm_d = nc.dram_tensor("sm", [depth, 128, 8], F32, kind="ExternalInput")
    lam_d = nc.dram_tensor("lam", [depth, 256], F32, kind="ExternalInput")
    rb_d = nc.dram_tensor("rb", [32, 4], F32, kind="ExternalInput")
    cst_d = nc.dram_tensor("cst", [3, 128, 128], F32, kind="ExternalInput")
    ohr_d = nc.dram_tensor("ohr", [32, 2048], F32, kind="ExternalInput")
    rope_d = nc.dram_tensor("rope", [2, 128, L], F32, kind="ExternalInput")
    y_d = nc.dram_tensor("y", [nseq, SEQ, DM], F32, kind="ExternalOutput")
    wb_d = nc.dram_tensor("wb", [NU, 128, 1024], BF16)
    wvb_d = nc.dram_tensor("wvb", [depth, 128, 5120], BF16)
    bvd_d = nc.dram_tensor("bvd", [4, 2048], BF16)

    P = Prog()
    es = ExitStack()
    with es:
        def sb(name, cols, dt):
            return es.enter_context(nc.sbuf_tensor(name, [128, cols], dt))

        def mkbuf(name, cols, dt):
            t = sb(name, cols, dt)
            return Buf(name, t, cols, dt, 0, dt, cols)

        xT = mkbuf("xT", NCH * L, F32)
        KVa = sb("KVa", 26528, BF16)
        ARa = sb("ARa", 23552, BF16)
        HK = mkbuf("HK", 4 * HKW, BF16)
        RC = mkbuf("RC", L, BF16)
        RS = mkbuf("RS", L, BF16)
        NR = 6
        RING = mkbuf("RING", NR * 1024, BF16)
        A0 = mkbuf("A0", 512, F32)
        A1 = mkbuf("A1", 512, F32)
        LN = mkbuf("LN", 512, F32)
        LN2 = mkbuf("LN2", 512, F32)
        SQ = mkbuf("SQ", 2 * 512, BF16)
        IDN = mkbuf("IDN", 128, F32)
        JB = mkbuf("JB", 128, BF16)
        BD = mkbuf("BD", 128, BF16)
        ON1 = mkbuf("ON1", 128, BF16)
        OND = mkbuf("OND", 128, BF16)
        ON128 = mkbuf("ON128", 128, BF16)
        GA = mkbuf("GA", depth * 32, F32)
        SM = mkbuf("SM", depth * 8, F32)
        LS = mkbuf("LS", depth * 8, F32)
        RBB = mkbuf("RBB", 128, F32)
        CZ = mkbuf("CZ", 4, F32)

        def sub(name, base, F_el, el_off, dt, ncols):
            return Buf(name, base, F_el, BF16, el_off * 2, dt, ncols)

        KA = sub("KVa", KVa, 26528, 0, BF16, 4 * L)
        KB = sub("KVa", KVa, 26528, 8256, BF16, 2 * L)
        VA = sub("KVa", KVa, 26528, 12384, BF16, 17 * 512)
        VB = sub("KVa", KVa, 26528, 21088, BF16, 17 * 320)
        UT = sub("KVa", KVa, 26528, 0, BF16, 32 * 512)
        XS = sub("KVa", KVa, 26528, 16384, F32, 2 * 1024)
        HTB = sub("ARa", ARa, 23552, 0, BF16, 8 * 512)
        YT = sub("ARa", ARa, 23552, 4096, F32, 8 * 512)
        QA = sub("ARa", ARa, 23552, 12288, BF16, 4 * 512)
        QB = sub("ARa", ARa, 23552, 14336, BF16, 4 * 512)
        OTB = sub("ARa", ARa, 23552, 16384, BF16, 8 * 512)
        PR = sub("ARa", ARa, 23552, 20480, BF16, 3 * 1024)
        WV = sub("ARa", ARa, 23552, 12288, BF16, 5120)
        HTB2 = sub("ARa", ARa, 23552, 17408, BF16, 8 * 512)
        STG = sub("ARa", ARa, 23552, 4096, F32, 4096)
        NPR = 3
        PSM = sub("ARa", ARa, 23552, 4096 + 7 * 1024, BF16, 1024)

        PSB = es.enter_context(nc.psum_tensor("psb", [128, 4096], F32))
        PS = [PSB[:, i * 512:(i + 1) * 512] for i in range(8)]
        RBT = mkbuf("RBT", 512, F32)


        def psr(i):
            return [("ps", i)]

        def dma(q, out, in_, reads, writes):
            P.add(q, lambda e: e.dma_start(out=out, in_=in_), reads, writes, dma=True)

        def mm(out, lhsT, rhs, start, stop, reads, writes):
            P.add("pe", lambda e: e.matmul(out, lhsT=lhsT, rhs=rhs, start=start, stop=stop), reads, writes)

        def act(out, in_, func, reads, writes, bias=None, scale=1.0):
            if func == AF.Copy:
                P.add("act", lambda e: e.activation(out=out, in_=in_, func=func), reads, writes)
                return
            if bias is None:
                bias = CZ.ap(0, 1, 0, out.shape[0])
            P.add("act", lambda e: e.activation(out=out, in_=in_, func=func, bias=bias, scale=scale),
                  list(reads) + CZ.pages(0, 4), writes)

        def tt(eng, out, in0, in1, op, reads, writes):
            P.add(eng, lambda e: e.tensor_tensor(out=out, in0=in0, in1=in1, op=op), reads, writes)

        def stt(eng, out, in0, scalar, in1, op0, op1, reads, writes):
            P.add(eng, lambda e: e.scalar_tensor_tensor(out=out, in0=in0, scalar=scalar, in1=in1, op0=op0, op1=op1),
                  reads, writes)

        def tsc(eng, out, in0, s1, op0, reads, writes, s2=None, op1=None):
            if op1 is None:
                P.add(eng, lambda e: e.tensor_scalar(out=out, in0=in0, scalar1=s1, scalar2=None, op0=op0), reads, writes)
            else:
                P.add(eng, lambda e: e.tensor_scalar(out=out, in0=in0, scalar1=s1, scalar2=s2, op0=op0, op1=op1),
                      reads, writes)

        def cpy(eng, out, in_, reads, writes):
            P.add(eng, lambda e: e.tensor_copy(out=out, in_=in_), reads, writes)

        def memset(eng, out, val, writes):
            P.add(eng, lambda e: e.memset(out, val), (), writes)

        memset("dve", CZ.ap(0, 1), 0.0, CZ.pages(0, 4))
        memset("dve", CZ.ap(1, 1), EPS, CZ.pages(0, 4))
        memset("dve", ON1.ap(0, 128), 1.0, ON1.pages(0, 128))
        memset("dve", OND.ap(0, 128), 1.0 / 1024.0, OND.pages(0, 128))
        memset("dve", ON128.ap(0, 128), 1.0 / 128.0, ON128.pages(0, 128))
        dma("sp", IDN.ap(0, 128), cst_d[0], [("cst",)], IDN.pages(0, 128))
        dma("sp", STG.ap(0, 128), cst_d[1], [("cst",)], STG.pages(0, 128))
        dma("sp", STG.ap(128, 128), cst_d[2], [("cst",)], STG.pages(128, 128))
        cpy("dve", JB.ap(0, 128), STG.ap(0, 128), STG.pages(0, 128), JB.pages(0, 128))
        cpy("dve", BD.ap(0, 128), STG.ap(128, 128), STG.pages(128, 128), BD.pages(0, 128))
        for l in range(depth):
            dma("sp", GA.ap(l * 32, 32), gains_d[l], [("gains",)], GA.pages(l * 32, 32))
            dma("sp", SM.ap(l * 8, 8), sm_d[l], [("sm",)], SM.pages(l * 8, 8))
        dma("sp", RBB.ap(0, 128), bass.AP(rb_d, 0, [[0, 128], [1, 128]]), [("rb",)], RBB.pages(0, 128))
        for i, RT in enumerate((RC, RS)):
            for c0 in range(0, L, 1032):
                dma("sp", STG.ap(1024, 1032), rope_d[i, :, c0:c0 + 1032], [("rope",)], STG.pages(1024, 1032))
                cpy("dve", RT.ap(c0, 1032), STG.ap(1024, 1032), STG.pages(1024, 1032), RT.pages(c0, 1032))
        for l in range(depth):
            lam_init = 0.8 - 0.6 * math.exp(-0.3 * l)
            dma("sp", STG.ap(2560, 256), bass.AP(lam_d, l * 256, [[0, 128], [1, 256]]), [("lam",)], STG.pages(2560, 256))
            tt("dve", STG.ap(2816, 64), STG.ap(2560, 64), STG.ap(2624, 64), ALU.mult, STG.pages(2560, 256), STG.pages(2816, 128))
            tt("dve", STG.ap(2880, 64), STG.ap(2688, 64), STG.ap(2752, 64), ALU.mult, STG.pages(2560, 256), STG.pages(2816, 128))
            P.add("dve", lambda e, o=LS.ap(l * 8 + 2, 1), i=STG.ap(2816, 64): e.reduce_sum(out=o, in_=i, axis=AX.X),
                  STG.pages(2816, 128), LS.pages(0, depth * 8))
            P.add("dve", lambda e, o=LS.ap(l * 8 + 3, 1), i=STG.ap(2880, 64): e.reduce_sum(out=o, in_=i, axis=AX.X),
                  STG.pages(2816, 128), LS.pages(0, depth * 8))
            act(LS.ap(l * 8 + 4, 2), LS.ap(l * 8 + 2, 2), AF.Exp, LS.pages(0, depth * 8), LS.pages(0, depth * 8))
            tt("dve", LS.ap(l * 8 + 0, 1), LS.ap(l * 8 + 5, 1), LS.ap(l * 8 + 4, 1), ALU.subtract,
               LS.pages(0, depth * 8), LS.pages(0, depth * 8))
            tsc("dve", LS.ap(l * 8 + 0, 1), LS.ap(l * 8 + 0, 1), -lam_init, ALU.add, LS.pages(0, depth * 8), LS.pages(0, depth * 8))
            tsc("dve", LS.ap(l * 8 + 1, 1), SM.ap(l * 8 + 0, 1), 1.0 - lam_init, ALU.mult,
                SM.pages(0, depth * 8) + LS.pages(0, depth * 8), LS.pages(0, depth * 8))
        dma("sp", STG.ap(3072, 4, 0, 32), rb_d[:, :], [("rb",)], STG.pages(3072, 4))
        for c in range(4):
            dma("sp", STG.ap(1024, 512, 0, 32), ohr_d[:, c * 512:(c + 1) * 512], [("ohr",)], STG.pages(1024, 512))
            mm(PS[c][0:4, :], STG.ap(3072, 4, 0, 32), STG.ap(1024, 512, 0, 32), True, True,
               STG.pages(3072, 4) + STG.pages(1024, 512), psr(c))
            cpy("dve", SQ.ap(0, 512, 0, 4), PS[c][0:4, :], psr(c), SQ.pages(0, 512))
            dma("sp", bvd_d[:, c * 512:(c + 1) * 512], SQ.ap(0, 512, 0, 4), SQ.pages(0, 512), [("bvd",)])
        band_offs = []
        mixed_const = set()
        for (qc, nq) in QBLK:
            segs = q_segments(qc, nq)
            for (kc, nk) in KTILES:
                sps = [tile_bias_spec(kc, nk, qc + so, sn) for so, sn in segs]
                uniform = all(sp[0] == "const" for sp in sps) and len(set(sp[1] for sp in sps)) == 1
                for (so, sn), sp_ in zip(segs, sps):
                    if sp_[0] == "band":
                        band_offs.append((sp_[1], sp_[1] + sn))
                    elif not uniform:
                        mixed_const.add((sp_[1], nk, sn))
        HKBASE = min(a for a, _ in band_offs)
        assert max(b for _, b in band_offs) - HKBASE <= HKW and HKBASE >= BVR_LO
        assert HKBASE + 127 + HKW <= BVR_LO + 2048
        const_off = {}
        for (bk_, nk_, sn_) in mixed_const:
            found = None
            for off in list(range(0, HKW - sn_ + 1)):
                lo = HKBASE + off
                hi = HKBASE + off + nk_ - 1 + sn_ - 1
                rels = (L - 1) - np.arange(lo, hi + 1)
                if (t5_bucket(rels) == bk_).all():
                    found = off
                    break
            assert found is not None, (bk_, nk_, sn_)
            const_off[(bk_, nk_, sn_)] = found
        for h in range(4):
            dma("sp", HK.ap(h * HKW, HKW),
                bass.AP(bvd_d, h * 2048 + HKBASE - BVR_LO, [[1, 128], [1, HKW]]), [("bvd",)], HK.pages(h * HKW, HKW))

        cast_engs = ["dve", "pool", "act"]
        conv_n = [0]
        YTB = [sub("ARa", ARa, 23552, 4096 + (4 + j) * 1024, BF16, 1024) for j in range(2)]

        def conv(item, late):
            kind, a, b = item
            src = wu_d[a] if kind == "u" else wv_d[a, :, b * 1024:(b + 1) * 1024]
            dstd = wb_d[a] if kind == "u" else wvb_d[a, :, b * 1024:(b + 1) * 1024]
            res = [("wb", a)] if kind == "u" else [("wvb", a)]
            i = conv_n[0]
            conv_n[0] += 1
            if not late:
                st_, stp = STG.ap((i % 4) * 1024, 1024), STG.pages((i % 4) * 1024, 1024)
                dst, dpg = RING.ap((i % NR) * 1024, 1024), RING.pages((i % NR) * 1024, 1024)
                dma("sp", st_, src, [("wsrc",)], stp)
                ce = cast_engs[i % 3]
                if ce == "act":
                    act(dst, st_, AF.Copy, stp, dpg)
                else:
                    cpy(ce, dst, st_, stp, dpg)
                dma("sp", dstd, dst, dpg, res)
            else:
                j = i % 2
                st_, stp = YT.ap(j * 1024, 1024), YT.pages(j * 1024, 1024)
                dma("pool", st_, src, [("wsrc",)], stp)
                conv_flush()
                late_pend.append((j, dstd, res))

        late_pend = []

        def conv_flush():
            if late_pend:
                j, dstd, res = late_pend.pop()
                st_, stp = YT.ap(j * 1024, 1024), YT.pages(j * 1024, 1024)
                dst, dpg = YTB[j].ap(0, 1024), YTB[j].pages(0, 1024)
                cpy("pool", dst, st_, stp, dpg)
                dma("pool", dstd, dst, dpg, res)

        early = [("u", u, 0) for u in range(min(28, NU))] + [("v", 0, k) for k in range(5)]
        late_items = [("u", u, 0) for u in range(28, NU)] + [("v", l, k) for l in range(1, depth) for k in range(5)]
        if not DBG.get("lateconv", 1):
            early, late_items = early + late_items, []
        for it in early:
            conv(it, False)
        for t in range(17):
            memset("pool", VB.ap(t * 320 + 64, 64), 1.0, VB.pages(t * 320 + 64, 64))
            memset("pool", VB.ap(t * 320 + 192, 64), 1.0, VB.pages(t * 320 + 192, 64))

        ring_ctr = [0]

        def load_unit(u):
            i = ring_ctr[0] % NR
            ring_ctr[0] += 1
            dma("sp", RING.ap(i * 1024, 1024), wb_d[u], [("wb", u)], RING.pages(i * 1024, 1024))
            return i

        def ring_w(i, j):
            return RING.ap(i * 1024 + j * 128, 128), RING.pages(i * 1024 + j * 128, 128)

        bank_ctr = [0]

        def next_bank(lo=0, hi=7):
            b = lo + bank_ctr[0] % (hi - lo)
            bank_ctr[0] += 1
            return b

        sq_ctr = [0]

        def xt(c, c0, n):
            return xT.ap(c * L + c0, n), xT.pages(c * L + c0, n)

        def stats_finish(n, scale_ln=0.0, LNt=None, bank=7):
            if LNt is None:
                LNt = (LN, 0)
            lb, lc = LNt
            la, lp = lb.ap(lc, n), lb.pages(lc, 512)
            act(la, PS[bank][:, 0:n], AF.Ln, psr(bank), lp, bias=CZ.ap(1, 1))
            if scale_ln == 0.0:
                act(la, la, AF.Exp, lp, lp, scale=-0.5)
            else:
                act(la, la, AF.Exp, lp, lp, bias=CZ.ap(2, 1), scale=-0.5)

        def prenorm(c0, n, gcol, l, HT, bank=7):
            for c in range(NCH):
                i = sq_ctr[0] % 2
                sq_ctr[0] += 1
                xa, xp = xt(c, c0, n)
                tt("pool", SQ.ap(i * 512, n), xa, xa, ALU.mult, xp, SQ.pages(i * 512, 512))
                mm(PS[bank][:, 0:n], OND.ap(0, 128), SQ.ap(i * 512, n), c == 0, c == NCH - 1,
                   SQ.pages(i * 512, 512) + OND.pages(0, 128), psr(bank))
            stats_finish(n, LNt=(LN2, 0), bank=bank)
            for c in range(NCH):
                xa, xp = xt(c, c0, n)
                stt("dve", HT.ap(c * 512, n), xa, GA.ap(l * 32 + gcol * 8 + c, 1), LN2.ap(0, n),
                    ALU.mult, ALU.mult, xp + LN2.pages(0, 512) + GA.pages(0, depth * 32), HT.pages(c * 512, n))

        def prenorm_staged(c0, n, gcol, l, HT, deferred, j0):
            def sqmm(c):
                def f():
                    i = sq_ctr[0] % 2
                    sq_ctr[0] += 1
                    xa, xp = xt(c, c0, n)
                    tt("pool", SQ.ap(i * 512, n), xa, xa, ALU.mult, xp, SQ.pages(i * 512, 512))
                    mm(PS[7][:, 0:n], OND.ap(0, 128), SQ.ap(i * 512, n), c == 0, c == NCH - 1,
                       SQ.pages(i * 512, 512) + OND.pages(0, 128), psr(7))
                return f
            for c in range(NCH):
                deferred.append((j0 + c, sqmm(c)))
            deferred.append((j0 + NCH + 1, lambda: stats_finish(n, LNt=(LN2, 0), bank=7)))

            def applies():
                for c in range(NCH):
                    xa, xp = xt(c, c0, n)
                    stt("dve", HT.ap(c * 512, n), xa, GA.ap(l * 32 + gcol * 8 + c, 1), LN2.ap(0, n),
                        ALU.mult, ALU.mult, xp + LN2.pages(0, 512) + GA.pages(0, depth * 32), HT.pages(c * 512, n))
            deferred.append((j0 + NCH + 3, applies))

        def proj_postnorm(src, nk_chunks, units, c0, n, gcol, l, bhi=7):
            pend = []

            def stat_mm(oc, i):
                mm(PS[7][:, 0:n], OND.ap(0, 128), SQ.ap(i * 512, n), oc == 0, oc == NCH - 1,
                   SQ.pages(i * 512, 512) + OND.pages(0, 128), psr(7))

            for oc in range(NCH):
                b = next_bank(0, bhi)
                nkk = len(units[oc]) * 8
                kk = 0
                for u in units[oc]:
                    ri = load_unit(u)
                    for j in range(8):
                        wa, wp = ring_w(ri, j)
                        mm(PS[b][:, 0:n], wa, src.ap(kk * 512, n), kk == 0, kk == nkk - 1,
                           wp + src.pages(kk * 512, n), psr(b))
                        kk += 1
                if pend:
                    stat_mm(*pend.pop())
                cpy("dve", YT.ap(oc * 512, n), PS[b][:, 0:n], psr(b), YT.pages(oc * 512, n))
                i = sq_ctr[0] % 2
                sq_ctr[0] += 1
                tt("pool", SQ.ap(i * 512, n), YT.ap(oc * 512, n), YT.ap(oc * 512, n), ALU.mult, YT.pages(oc * 512, n),
                   SQ.pages(i * 512, 512))
                pend.append((oc, i))
            stat_mm(*pend.pop())
            stats_finish(n)
            for oc in range(NCH):
                e = "pool"
                stt("dve", YT.ap(oc * 512, n), YT.ap(oc * 512, n), GA.ap(l * 32 + gcol * 8 + oc, 1), LN.ap(0, n),
                    ALU.mult, ALU.mult, YT.pages(oc * 512, n) + LN.pages(0, 512), YT.pages(oc * 512, n))
                xa, xp = xt(oc, c0, n)
                tt(e, xa, xa, YT.ap(oc * 512, n), ALU.add, xp + YT.pages(oc * 512, n), xp)

        rope_ctr = [0]

        def rope_chain(bx, by, n, c0, gcol, l, dst, dpg, is_q):
            k = rope_ctr[0] % 2
            rope_ctr[0] += 1
            t0, t1, tl = 3 * k, 3 * k + 1, 3 * k + 2
            T0, T0p = YT.ap(t0 * 512, n), YT.pages(t0 * 512, 512)
            T1, T1p = YT.ap(t1 * 512, n), YT.pages(t1 * 512, 512)
            TL, TLp = YT.ap(tl * 512, n), YT.pages(tl * 512, 512)
            i = sq_ctr[0] % 2
            sq_ctr[0] += 1
            act(SQ.ap(i * 512, n), PS[bx][:, 0:n], AF.Square, psr(bx), SQ.pages(i * 512, 512))
            mm(PS[7][:, 0:n], BD.ap(0, 128), SQ.ap(i * 512, n), True, True, SQ.pages(i * 512, 512) + BD.pages(0, 128), psr(7))
            stats_finish(n, scale_ln=(1.0 if is_q else 0.0), LNt=(YT, tl * 512))
            stt("dve", T0, PS[bx][:, 0:n], SM.ap(l * 8 + gcol, 1), RC.ap(c0, n), ALU.mult, ALU.mult,
                psr(bx) + RC.pages(c0, n) + SM.pages(0, depth * 8), T0p)
            stt("dve", T1, PS[by][:, 0:n], SM.ap(l * 8 + gcol + 1, 1), RS.ap(c0, n), ALU.mult, ALU.mult,
                psr(by) + RS.pages(c0, n) + SM.pages(0, depth * 8), T1p)
            tt("pool", T0, T0, T1, ALU.add, T0p + T1p, T0p)
            tt("pool", dst, T0, TL, ALU.mult, T0p + TLp, dpg)

        memset("dve", CZ.ap(2, 1), math.log(0.125), CZ.pages(0, 4))

        def proj_chunk(HT, u, n):
            ri = load_unit(u)
            b = next_bank()
            for j in range(8):
                wa, wp = ring_w(ri, j)
                mm(PS[b][:, 0:n], wa, HT.ap(j * 512, n), j == 0, j == 7, wp + HT.pages(j * 512, n), psr(b))
            return b

        stages = {None: 99, "in": 0, "kv": 1, "attn": 2, "l0": 3}
        stop_lvl = stages[stop_after]
        tr_ctr = [0]
        loaded = set()
        stored = set()
        pend_slot = [0]

        def load_tile(s, t, q_, bhi):
            loaded.add((s, t))
            k = tr_ctr[0] % 2
            tr_ctr[0] += 1
            nk = 128 if t < 16 else 16
            src = x_d[s, t * 128:(t + 1) * 128, :] if t < 16 else meta_d[:, :]
            dma(q_, XS.ap(k * 1024, 1024, 0, nk), src, [("x", s)], XS.pages(k * 1024, 1024))
            for half in range(2):
                b = next_bank(0, bhi)
                for q in range(4):
                    c = half * 4 + q
                    P.add("pe", lambda e, o=PS[b][:, q * 128:q * 128 + nk], i=XS.ap(k * 1024 + c * 128, 128, 0, nk),
                          idn=IDN.ap(0, nk, 0, nk): e.transpose(o, i, idn),
                          XS.pages(k * 1024 + c * 128, 128) + IDN.pages(0, 128), psr(b))
                o3 = xT.a3((half * 4) * L + t * 128, [[L, 4], [1, nk]])
                i3 = bass.AP(PSB, b * 512, [[4096, 128], [128, 4], [1, nk]])
                pg = []
                for q in range(4):
                    pg += xT.pages((half * 4 + q) * L + t * 128, nk)
                cpy("dve", o3, i3, psr(b), pg)

        def store_tile(s, t, q_, bhi, pend):
            k = tr_ctr[0] % 2
            tr_ctr[0] += 1
            for half in range(2):
                b = next_bank(0, bhi)
                for q in range(4):
                    c = half * 4 + q
                    xa, xp = xt(c, t * 128, 128)
                    P.add("pe", lambda e, o=PS[b][:, q * 128:(q + 1) * 128], i=xa, idn=IDN.ap(0, 128): e.transpose(o, i, idn),
                          xp + IDN.pages(0, 128), psr(b))
                cpy("dve", XS.ap(k * 1024 + half * 512, 512), PS[b][:, :], psr(b), XS.pages(k * 1024 + half * 512, 512))

            def st_(s=s, t=t, k=k):
                dma(q_, y_d[s, t * 128:(t + 1) * 128, :], XS.ap(k * 1024, 1024), XS.pages(k * 1024, 1024), [("y", s, t)])
            if pend is None:
                st_()
            else:
                pend.append(st_)
                pend_slot[0] = k

        for s in range(nseq):
            for t in range(17):
                if (s, t) not in loaded:
                    load_tile(s, t, "act", 7)
            for l in range(depth):
                if stop_lvl < 1:
                    break
                P.new_epoch()
                ub = l * UPL
                dma("sp", WV.ap(0, 5120), wvb_d[l], [("wvb", l)], WV.pages(0, 5120))
                prenorm(BLOCKS[0][0], BLOCKS[0][1], 0, l, HTB)
                for bi, (c0, n) in enumerate(BLOCKS):
                    HT = HTB if bi % 2 == 0 else HTB2
                    kvdef = []
                    if bi + 1 < len(BLOCKS):
                        prenorm_staged(BLOCKS[bi + 1][0], BLOCKS[bi + 1][1], 0, l, HTB2 if bi % 2 == 0 else HTB, kvdef, 0)
                    ntile = (n + 127) // 128
                    per = -(-len(kvdef) // ntile) if kvdef else 0
                    for h in range(4 if "k" in kvp else 0):
                        b = proj_chunk(HT, ub + U_AK + h, n)
                        cpy("dve", KA.ap(h * L + c0, n), PS[b][:, 0:n], psr(b), KA.pages(h * L + c0, n))
                    for g in range(2 if "r" in kvp else 0):
                        bx = proj_chunk(HT, ub + U_BK + g, n)
                        by = proj_chunk(HT, ub + U_BKS + g, n)
                        rope_chain(bx, by, n, c0, 3, l, KB.ap(g * L + c0, n), KB.pages(g * L + c0, n), False)
                    for tc in range(c0, c0 + n if "v" in kvp else c0, 128):
                        t = tc // 128
                        nk = min(128, c0 + n - tc)
                        b = next_bank()
                        for j in range(8):
                            mm(PS[b][0:nk, :], HT.ap(j * 512 + tc - c0, nk), WV.ap(j * 640, 512), j == 0, j == 7,
                               HT.pages(j * 512 + tc - c0, nk) + WV.pages(j * 640, 512), psr(b))
                        cpy("dve", VA.ap(t * 512, 512, 0, nk), PS[b][0:nk, :], psr(b), VA.pages(t * 512, 512))
                        b = next_bank()
                        for j in range(8):
                            mm(PS[b][0:nk, 0:128], HT.ap(j * 512 + tc - c0, nk), WV.ap(j * 640 + 512, 128), j == 0, j == 7,
                               HT.pages(j * 512 + tc - c0, nk) + WV.pages(j * 640 + 512, 128), psr(b))
                        o3 = VB.a3(t * 320, [[128, 2], [1, 64]], 0, nk)
                        i3 = bass.AP(PSB, b * 512, [[4096, nk], [64, 2], [1, 64]])
                        cpy("dve", o3, i3, psr(b), VB.pages(t * 320, 64) + VB.pages(t * 320 + 128, 64))
                        cpy("dve", VB.ap(t * 320 + 256, 64, 0, nk), PS[b][0:nk, 0:64], psr(b), VB.pages(t * 320 + 256, 64))
                        for _ in range(per):
                            if kvdef:
                                kvdef.pop(0)[1]()
                    while kvdef:
                        kvdef.pop(0)[1]()
                if stop_lvl < 2:
                    continue
                pr_ctr = [0]
                acc_ctr = [0]
                s_ctr = [0]
                def qproj_a(c0, n):
                    for h in range(4):
                        b = proj_chunk(HTB, ub + U_AQ + h, n)
                        tsc("dve", QA.ap(h * 512, n), PS[b][:, 0:n], 0.125, ALU.mult, psr(b), QA.pages(h * 512, n))

                def qproj_b(c0, n):
                    for c in range(4):
                        bx = proj_chunk(HTB, ub + U_BQ + c, n)
                        by = proj_chunk(HTB, ub + U_BQS + c, n)
                        rope_chain(bx, by, n, c0, 1, l, QB.ap(c * 512, n), QB.pages(c * 512, n), True)

                prenorm(QBLK[0][0], QBLK[0][1], 0, l, HTB)
                qproj_a(*QBLK[0])
                qproj_b(*QBLK[0])
                for bi, (c0, n) in enumerate(QBLK):
                    groups = []
                    for i in range(8):
                        groups.append(("A", i // 2, i % 2))
                        groups.append(("B", i, 0))
                    kpairs = [[KTILES[2 * i], KTILES[2 * i + 1]] for i in range(8)] + [[KTILES[16]]]
                    jobs = []
                    for gi, g in enumerate(groups):
                        for pi_, kp in enumerate(kpairs):
                            jobs.append((gi, g, pi_, kp))
                    LA = 2

                    def emit_qk(job):
                        gi, g, pj, kp = job
                        kind, h, m = g
                        sl = s_ctr[0] % 2
                        s_ctr[0] += 1
                        pi = pr_ctr[0] % NPR
                        pr_ctr[0] += 1
                        specs = []
                        segs = q_segments(c0, n)
                        for ti, (kc, nk) in enumerate(kp):
                            sb_ = 2 * sl + ti
                            if kind == "A":
                                r0 = 64 * m
                                kap, kpg = KA.ap(h * L + kc, nk, r0, r0 + 64), KA.pages(h * L + kc, nk)
                                qb_, qoff = QA, h * 512
                                sps = [tile_bias_spec(kc, nk, c0 + so, sn) for so, sn in segs]
                                uniform = all(sp[0] == "const" for sp in sps) and len(set(sp[1] for sp in sps)) == 1
                                spec = sps[0] if uniform else ("band", 0)
                            else:
                                r0 = 64 * (h % 2)
                                gk = h // 4
                                kap, kpg = KB.ap(gk * L + kc, nk, r0, r0 + 64), KB.pages(gk * L + kc, nk)
                                qb_, qoff = QB, (h // 2) * 512
                                sps = None
                                spec = ("none", 0)
                            specs.append(spec)
                            if spec[0] != "band":
                                mm(PS[sb_][0:nk, 0:n], kap, qb_.ap(qoff, n, r0, r0 + 64), True, True,
                                   kpg + qb_.pages(qoff, n), psr(sb_))
                            else:
                                for (so, sn), sp_ in zip(segs, sps):
                                    mm(PS[sb_][0:nk, so:so + sn], kap, qb_.ap(qoff + so, sn, r0, r0 + 64), True, False,
                                       kpg + qb_.pages(qoff + so, sn), psr(sb_))
                                    off = (sp_[1] - HKBASE) if sp_[0] == "band" else const_off[(sp_[1], nk, sn)]
                                    mm(PS[sb_][0:nk, so:so + sn], JB.ap(128 - nk, nk), HK.ap(h * HKW + off, sn), False, True,
                                       JB.pages(0, 128) + HK.pages(h * HKW + off, sn), psr(sb_))

                        def bias_of(spec, nk):
                            if spec[0] == "const":
                                return RBB.ap(spec[1] * 4 + h, 1, 0, nk)
                            return CZ.ap(0, 1, 0, nk)

                        def bkey(spec):
                            return spec[1] if spec[0] == "const" else -1

                        if len(kp) == 2 and bkey(specs[0]) == bkey(specs[1]):
                            src = bass.AP(PSB, 2 * sl * 512, [[4096, 128], [512, 2], [1, n]])
                            dst = PR.a3(pi * 1024, [[512, 2], [1, n]])
                            act(dst, src, AF.Exp, psr(2 * sl) + psr(2 * sl + 1) + RBB.pages(0, 128),
                                PR.pages(pi * 1024, 1024), bias=bias_of(specs[0], 128))
                        else:
                            for ti, (kc, nk) in enumerate(kp):
                                sb_ = 2 * sl + ti
                                act(PR.ap(pi * 1024 + ti * 512, n, 0, nk), PS[sb_][0:nk, 0:n], AF.Exp,
                                    psr(sb_) + RBB.pages(0, 128), PR.pages(pi * 1024 + ti * 512, 512),
                                    bias=bias_of(specs[ti], nk))
                        return pi

                    def emit_pv(job, pi):
                        gi, g, pj, kp = job
                        kind, h, m = g
                        for ti, (kc, nk) in enumerate(kp):
                            t = kc // 128
                            first = pj == 0 and ti == 0
                            last = pj == len(kpairs) - 1 and ti == len(kp) - 1
                            pap, ppg = PR.ap(pi * 1024 + ti * 512, n, 0, nk), PR.pages(pi * 1024 + ti * 512, 512)
                            if kind == "A":
                                bo, bs = 4, 5
                                p2 = DBG.get("p2", 0)
                                if p2 and len(kp) == 2 and ti == 0:
                                    ps_ = psm_ctr[0] % 2
                                    psm_ctr[0] += 1
                                    tt(DBG.get("p2eng", "pool"), PSM.ap(ps_ * 512, n), PR.ap(pi * 1024, n), PR.ap(pi * 1024 + 512, n), ALU.add,
                                       PR.pages(pi * 1024, 1024), PSM.pages(ps_ * 512, 512))
                                    if pend_ones:
                                        pend_ones.pop()()

                                    def ones_mm(ps_=ps_, st_=(pj == 0)):
                                        mm(PS[bs][:, 0:n], ON1.ap(0, 128), PSM.ap(ps_ * 512, n), st_, False,
                                           ON1.pages(0, 128) + PSM.pages(ps_ * 512, 512), psr(bs))
                                    pend_ones.append(ones_mm)
                                if p2 and len(kp) == 1 and pend_ones:
                                    pend_ones.pop()()
                                mm(PS[bo][:, 0:n], VA.ap(t * 512 + h * 128, 128, 0, nk), pap, first, last,
                                   VA.pages(t * 512 + h * 128, 128) + ppg, psr(bo))
                                if not p2:
                                    mm(PS[bs][:, 0:n], ON1.ap(0, 128, 0, nk), pap, first, last, ON1.pages(0, 128) + ppg, psr(bs))
                                elif len(kp) == 1:
                                    mm(PS[bs][:, 0:n], ON1.ap(0, 128, 0, nk), pap, False, last, ON1.pages(0, 128) + ppg, psr(bs))
                            else:
                                bo = 6
                                gk = h // 4
                                odd = h % 2
                                if odd == 0:
                                    vc = 0 if gk == 0 else 128
                                else:
                                    vc = 192 if gk == 0 else 64
                                mm(PS[bo][:, 0:n], VB.ap(t * 320 + vc, 128, 0, nk), pap, first, last,
                                   VB.pages(t * 320 + vc, 128) + ppg, psr(bo))
                        if pj != len(kpairs) - 1:
                            return
                        if kind == "A":
                            bo, bs = 4, 5
                            AT = A0 if m == 0 else A1
                            P.add("dve", lambda e, o=AT.ap(0, n), i=PS[bs][:, 0:n]: (e.reciprocal_approx_fast(out=o, in_=i) if DBG.get('rfast', 0) else e.reciprocal(out=o, in_=i)),
                                  psr(bs), AT.pages(0, 512))
                            tt("dve", AT.ap(0, n), PS[bo][:, 0:n], AT.ap(0, n), ALU.mult, psr(bo) + AT.pages(0, 512),
                               AT.pages(0, 512))
                            if m == 1:
                                def stage2(h=h):
                                    stt("dve", A0.ap(0, n), A1.ap(0, n), LS.ap(l * 8 + 0, 1), A0.ap(0, n), ALU.mult, ALU.add,
                                        A0.pages(0, 512) + A1.pages(0, 512) + LS.pages(0, depth * 8), A0.pages(0, 512))
                                    i = sq_ctr[0] % 2
                                    sq_ctr[0] += 1
                                    tt("pool", SQ.ap(i * 512, n), A0.ap(0, n), A0.ap(0, n), ALU.mult, A0.pages(0, 512),
                                       SQ.pages(i * 512, 512))
                                    sq_slot[0] = i

                                def stage2b(h=h):
                                    i = sq_slot[0]
                                    mm(PS[7][:, 0:n], ON128.ap(0, 128), SQ.ap(i * 512, n), True, True,
                                       SQ.pages(i * 512, 512) + ON128.pages(0, 128), psr(7))

                                def stage3(h=h):
                                    stats_finish(n)

                                def stage4(h=h):
                                    stt("dve", OTB.ap(h * 512, n), A0.ap(0, n), LS.ap(l * 8 + 1, 1), LN.ap(0, n), ALU.mult, ALU.mult,
                                        A0.pages(0, 512) + LN.pages(0, 512) + LS.pages(0, depth * 8), OTB.pages(h * 512, n))
                                deferred.append((cur_step[0] + 1, stage2))
                                deferred.append((cur_step[0] + 6, stage2b))
                                deferred.append((cur_step[0] + 8, stage3))
                                deferred.append((cur_step[0] + 10, stage4))
                        else:
                            bo = 6
                            odd = h % 2
                            orow, srow = (0, 64) if odd == 0 else (64, 0)
                            P.add("dve", lambda e, o=RBT.ap(0, n, srow, srow + 64), i=PS[bo][srow:srow + 64, 0:n]:
                                  (e.reciprocal_approx_fast(out=o, in_=i) if DBG.get('rfast', 0) else e.reciprocal(out=o, in_=i)), psr(bo), RBT.pages(0, 512))
                            tt("dve", OTB.ap((4 + h // 2) * 512, n, orow, orow + 64), PS[bo][orow:orow + 64, 0:n],
                               RBT.ap(0, n, srow, srow + 64), ALU.mult, psr(bo) + RBT.pages(0, 512),
                               OTB.pages((4 + h // 2) * 512, n))

                    deferred = []
                    cur_step = [0]
                    pend_ones = []
                    sq_slot = [0]
                    psm_ctr = [0]
                    if bi + 1 < len(QBLK) and DBG.get("p1", 1):
                        prenorm_staged(QBLK[bi + 1][0], QBLK[bi + 1][1], 0, l, HTB, deferred, 38)

                    def run_deferred(upto):
                        keep = []
                        for due, fn in deferred:
                            if due <= upto:
                                fn()
                            else:
                                keep.append((due, fn))
                        deferred[:] = keep

                    pend = []
                    for st in range(len(jobs) + LA):
                        cur_step[0] = st
                        run_deferred(st)
                        if late_items and st % 4 == 1:
                            conv(late_items.pop(0), True)
                        if st < len(jobs):
                            pend.append(emit_qk(jobs[st]))
                        if st >= LA:
                            emit_pv(jobs[st - LA], pend[st - LA])
                    conv_flush()
                    if bi + 1 < len(QBLK):
                        run_deferred(len(jobs) + LA)
                        qproj_a(*QBLK[bi + 1])
                    run_deferred(10 ** 9)
                    proj_postnorm(OTB, 8, [[ub + U_WO + oc] for oc in range(NCH)], c0, n, 1, l)
                    if bi + 1 < len(QBLK):
                        qproj_b(*QBLK[bi + 1])
                if stop_lvl < 3:
                    continue
                while late_items:
                    conv(late_items.pop(0), True)
                conv_flush()
                MB = [(0, 416), (416, 416), (832, 416), (1248, 416), (1664, 400)]
                io_pend = []
                prenorm(MB[0][0], MB[0][1], 2, l, HTB, bank=6)
                for bi, (c0, n) in enumerate(MB):
                    HT = HTB if bi % 2 == 0 else HTB2
                    for f in range(32):
                        ri = load_unit(ub + U_UP + f)
                        b = next_bank(0, 6)
                        for j in range(8):
                            wa, wp = ring_w(ri, j)
                            mm(PS[b][:, 0:n], wa, HT.ap(j * 512, n), j == 0, j == 7, wp + HT.pages(j * 512, n), psr(b))
                        yi = f % 8
                        act(YT.ap(yi * 512, n), PS[b][:, 0:n], AF.Relu, psr(b), YT.pages(yi * 512, n))
                        tt("pool", UT.ap(f * 512, n), YT.ap(yi * 512, n), YT.ap(yi * 512, n), ALU.mult,
                           YT.pages(yi * 512, n), UT.pages(f * 512, n))
                    if bi + 1 < len(MB):
                        prenorm(MB[bi + 1][0], MB[bi + 1][1], 2, l, HTB2 if bi % 2 == 0 else HTB, bank=6)
                    proj_postnorm(UT, 32, [[ub + U_DN + oc * 4 + kg for kg in range(4)] for oc in range(NCH)], c0, n, 3, l, bhi=6)
                    if l == depth - 1 and stop_lvl == 99 and DBG.get("iolap", 0):
                        while io_pend:
                            io_pend.pop(0)()
                        for t in range(17):
                            if (t + 1) * 128 <= c0 + n or (t == 16 and c0 + n == L):
                                if t < 16 and (s, t) not in stored:
                                    while io_pend:
                                        io_pend.pop(0)()
                                    store_tile(s, t, "sp", 6, io_pend)
                                    stored.add((s, t))
                                if s + 1 < nseq and (s + 1, t) not in loaded and (t == 16 or (s, t) in stored):
                                    if io_pend and tr_ctr[0] % 2 == pend_slot[0]:
                                        while io_pend:
                                            io_pend.pop(0)()
                                    load_tile(s + 1, t, "sp", 6)
                if l == depth - 1:
                    while io_pend:
                        io_pend.pop(0)()
                if stop_lvl == 3:
                    break
            for t in range(16):
                if (s, t) not in stored:
                    store_tile(s, t, "act", 7, None)
                    stored.add((s, t))

        keys = P.finalize()
        sems = {k: es.enter_context(nc.semaphore(f"s{i}")) for i, k in enumerate(keys)}
        final = {}
        for q in P.DMAK:
            for op in P.dma_hist[q]:
                final[op.key] = (q, op.sigval)
        blk = es.enter_context(nc.Block())

        def emit(e, name):
            for op in P.streams[name]:
                for d in op.waits:
                    e.wait_ge(sems[d.key], d.sigval)
                ins = op.fn(e)
                if op.signal:
                    ins.then_inc(sems[op.key], 16 if op.is_dma else 1)
            for k, (q, v) in final.items():
                if q == name:
                    e.wait_ge(sems[k], v)

        @blk.tensor
        def _(e):
            emit(e, "pe")

        @blk.scalar
        def _(e):
            emit(e, "act")

        @blk.vector
        def _(e):
            emit(e, "dve")

        @blk.gpsimd
        def _(e):
            emit(e, "pool")

        @blk.sync
        def _(e):
            emit(e, "sp")
    nc._prog_stats = {e: len(P.streams[e]) for e in P.ENG}
    nc._nsem = len(keys)
    return nc


def _unit(wblk):
    return np.ascontiguousarray(wblk.reshape(8, 128, 128).transpose(1, 0, 2).reshape(128, 1024))


def _pairswap(w):
    s = w.shape
    return w.reshape(s[0], s[1] // 2, 2)[:, :, ::-1].reshape(s)


def host_prep(inp, depth=DEPTH):
    f = lambda a: np.asarray(a, dtype=np.float32)
    w_in, w_out, w_up, w_down = f(inp["w_in"]), f(inp["w_out"]), f(inp["w_up"]), f(inp["w_down"])
    wu = np.empty((depth * UPL, 128, 1024), np.float32)
    wv = np.empty((depth, 128, 5120), np.float32)
    for l in range(depth):
        W = w_in[l]
        b = l * UPL
        for h in range(4):
            wu[b + U_AQ + h] = _unit(W[:, h * 128:(h + 1) * 128])
            wu[b + U_AK + h] = _unit(W[:, 512 + h * 128:512 + (h + 1) * 128])
            bq = W[:, 1536 + h * 128:1536 + (h + 1) * 128]
            wu[b + U_BQ + h] = _unit(bq)
            wu[b + U_BQS + h] = _unit(_pairswap(bq))
        for g in range(2):
            bk = W[:, 2048 + g * 64:2048 + (g + 1) * 64]
            wu[b + U_BK + g] = _unit(np.concatenate([bk, bk], axis=1))
            bks = _pairswap(bk)
            wu[b + U_BKS + g] = _unit(np.concatenate([bks, bks], axis=1))
        for oc in range(8):
            wu[b + U_WO + oc] = _unit(w_out[l][:, oc * 128:(oc + 1) * 128])
        for fch in range(32):
            wu[b + U_UP + fch] = _unit(w_up[l][:, fch * 128:(fch + 1) * 128])
        for oc in range(8):
            for kg in range(4):
                wu[b + U_DN + oc * 4 + kg] = _unit(w_down[l][kg * 1024:(kg + 1) * 1024, oc * 128:(oc + 1) * 128])
        vcols = np.concatenate([W[:, 1024:1536], W[:, 2176:2304]], axis=1)
        wv[l] = vcols.reshape(8, 128, 640).transpose(1, 0, 2).reshape(128, 5120)
    gains = np.empty((depth, 128, 32), np.float32)
    for l in range(depth):
        for k, nm in enumerate(("g_attn_pre", "g_attn_post", "g_mlp_pre", "g_mlp_post")):
            gains[l, :, k * 8:(k + 1) * 8] = f(inp[nm])[l].reshape(8, 128).T
    sm = np.zeros((depth, 128, 8), np.float32)
    idx = np.arange(128) % 64
    swp = idx ^ 1
    for l in range(depth):
        sm[l, :, 0] = f(inp["g_subln"])[l]
        sm[l, :, 1] = f(inp["g_qnorm"])[l][idx]
        sm[l, :, 2] = f(inp["g_qnorm"])[l][swp]
        sm[l, :, 3] = f(inp["g_knorm"])[l][idx]
        sm[l, :, 4] = f(inp["g_knorm"])[l][swp]
    lam = np.stack([np.concatenate([f(inp[k])[l] for k in ("lambda_q1", "lambda_k1", "lambda_q2", "lambda_k2")])
                    for l in range(depth)])
    cst = np.zeros((3, 128, 128), np.float32)
    cst[0] = np.eye(128)
    cst[1] = np.eye(128)[::-1]
    cst[2] = np.kron(np.eye(2), np.ones((64, 64))) / 64.0
    j = np.arange(BVR_LO, BVR_LO + 2048)
    bk = t5_bucket(L - 1 - j)
    ohr = (bk[None, :] == np.arange(32)[:, None]).astype(np.float32)
    t = np.arange(SEQ)
    row = (t // 64).astype(np.float32)
    col = (t % 64).astype(np.float32)
    inv = (np.float32(10000.0) ** (-np.arange(0, 32, 2, dtype=np.float32) / np.float32(32))).astype(np.float32)
    ang = np.concatenate([row[:, None] * inv[None, :], col[:, None] * inv[None, :]], axis=-1).astype(np.float32)
    ang = np.concatenate([ang, np.zeros((NMETA, 32), np.float32)], axis=0)
    cosT = np.cos(ang).astype(np.float32).T
    sinT = np.sin(ang).astype(np.float32).T
    p = np.arange(128)
    i = (p % 64) // 2
    sign = np.where(p % 2 == 0, -1.0, 1.0).astype(np.float32)
    rope = np.stack([cosT[i], sinT[i] * sign[:, None]]).astype(np.float32)
    return dict(meta=f(inp["meta_tokens"]), wu=wu, wv=wv, gains=gains, sm=sm, lam=lam.astype(np.float32),
                rb=f(inp["rel_bias"]), cst=cst, ohr=ohr, rope=np.ascontiguousarray(rope))


_NC_CACHE = {}


def kernel(**inputs):
    ncores = 8
    xp = np.asarray(inputs["x_prompt"], dtype=np.float32)
    xs = np.asarray(inputs["x_sample"], dtype=np.float32)
    xall = np.concatenate([xp, xs], axis=0)
    nseq = xall.shape[0] // ncores
    shared = host_prep(inputs)
    if nseq not in _NC_CACHE:
        _NC_CACHE[nseq] = build_nc(nseq)
    nc = _NC_CACHE[nseq]
    in_maps = []
    for c in range(ncores):
        m = dict(shared)
        m["x"] = np.ascontiguousarray(xall[c * nseq:(c + 1) * nseq])
        in_maps.append(m)
    res = run_bass_kernel_spmd(nc, in_maps, core_ids=list(range(ncores)))
    yall = np.concatenate([r["y"] for r in res.results], axis=0)
    nb = xp.shape[0]
    return (np.ascontiguousarray(yall[:nb]), np.ascontiguousarray(yall[nb:]))
```

```python
import math
from contextlib import ExitStack
import numpy as np
import concourse.bass as bass
import concourse.mybir as mybir
from concourse.bass_utils import run_bass_kernel_spmd

F32 = mybir.dt.float32
BF16 = mybir.dt.bfloat16
AF = mybir.ActivationFunctionType
ALU = mybir.AluOpType
AX = mybir.AxisListType

DM = 1024
SEQ = 2048
NMETA = 16
L = SEQ + NMETA
DEPTH = 2
NCH = 8
EPS = 1e-6
BLOCKS = [(0, 512), (512, 512), (1024, 512), (1536, 512), (2048, 16)]
KTILES = [(t * 128, 128) for t in range(16)] + [(2048, 16)]
UPL = 92
U_AQ, U_AK, U_BQ, U_BQS, U_BK, U_BKS, U_WO, U_UP, U_DN = 0, 4, 8, 12, 16, 18, 20, 28, 60
BVR_LO = 1024
HKW = 1152
PG = 256


def pos0(col):
    return NMETA + col if col < SEQ else col - SEQ


def t5_bucket(rel):
    nb, me = 16, 8
    base = np.where(rel > 0, nb, 0)
    n = np.abs(rel)
    nf = np.maximum(n, 1).astype(np.float32)
    large = me + (np.log(nf / np.float32(me)) / np.float32(math.log(128 / 8)) * np.float32(nb - me)).astype(np.int32)
    large = np.minimum(large, nb - 1)
    return base + np.where(n < me, n, large)


QBLK = list(BLOCKS)


def q_segments(c0, n):
    if c0 + n <= SEQ or c0 >= SEQ:
        return [(0, n)]
    return [(0, SEQ - c0), (SEQ - c0, c0 + n - SEQ)]


def tile_bias_spec(kcol, nk, qcol, nq):
    D = pos0(kcol) - pos0(qcol)
    rel = D + np.arange(nk)[:, None] - np.arange(nq)[None, :]
    b = t5_bucket(rel)
    if (b == b.flat[0]).all():
        return ("const", int(b.flat[0]))
    return ("band", L - nk - D)


class Op:
    __slots__ = ("eng", "fn", "waits", "signal", "key", "seq", "clock", "clock2", "is_dma", "sigval", "idx")


class Prog:
    ENG = ("pe", "act", "dve", "pool", "sp")
    DMAK = {"sp": 8, "act": 4, "pool": 2}

    def __init__(self):
        self.streams = {e: [] for e in self.ENG}
        self.seen = {e: {} for e in self.ENG}
        self.lastw = {}
        self.readers = {}
        self.epoch = 0
        self.cnt = {}
        self.dma_n = {q: 0 for q in self.DMAK}
        self.dma_hist = {q: [] for q in self.DMAK}
        self.done = {e: {} for e in self.ENG}
        self.nops = 0

    def new_epoch(self):
        for e in self.ENG:
            k = (e, self.epoch)
            if k in self.cnt:
                self.done[e] = dict(self.done[e])
                self.done[e][k] = self.cnt[k] - 1
        self.epoch += 1

    def add(self, eng, fn, reads=(), writes=(), dma=False):
        op = Op()
        op.eng, op.fn, op.is_dma, op.signal, op.sigval = eng, fn, dma, dma, 0
        op.idx = self.nops
        self.nops += 1
        deps = {}
        lastw, readers = self.lastw, self.readers
        if eng != "pe" and reads:
            pr = [r for r in reads if r[0] == "ps"]
            if pr:
                writes = list(writes) + pr
                reads = [r for r in reads if r[0] != "ps"]
        for r in reads:
            w = lastw.get(r)
            if w is not None:
                deps[w.idx] = w
        for r in writes:
            w = lastw.get(r)
            if w is not None:
                deps[w.idx] = w
            rs = readers.get(r)
            if rs:
                for o in rs:
                    deps[o.idx] = o
        if dma:
            K = self.DMAK[eng]
            i = self.dma_n[eng]
            if i >= K:
                o = self.dma_hist[eng][i - K]
                deps[o.idx] = o
        elif eng == "pool" and DBG.get("poolser", 0):
            for o in reversed(self.streams[eng]):
                if not o.is_dma:
                    deps[o.idx] = o
                    break
        seen = self.seen[eng]
        waits = []
        cow = False
        for di in sorted(deps):
            d = deps[di]
            if (not d.is_dma) and d.eng == eng and eng == "pe":
                continue
            if seen.get(d.key, -1) >= d.seq:
                continue
            waits.append(d)
            d.signal = True
            if not cow:
                seen = dict(seen)
                self.seen[eng] = seen
                cow = True
            seen[d.key] = d.seq
            for kk, ss in d.clock.items():
                if seen.get(kk, -1) < ss:
                    seen[kk] = ss
            for kk, ss in d.clock2.items():
                if seen.get(kk, -1) < ss:
                    seen[kk] = ss
        op.waits = waits
        if dma:
            K = self.DMAK[eng]
            i = self.dma_n[eng]
            self.dma_n[eng] = i + 1
            op.key = ("dma", eng, i % K)
            op.seq = i // K + 1
            self.dma_hist[eng].append(op)
            op.clock = seen
            op.clock2 = {}
        else:
            op.key = (eng, self.epoch)
            op.seq = self.cnt.get(op.key, 0)
            self.cnt[op.key] = op.seq + 1
            op.clock = seen
            op.clock2 = self.done[eng]
        for r in reads:
            readers.setdefault(r, []).append(op)
        for r in writes:
            lastw[r] = op
            readers[r] = []
        self.streams[eng].append(op)
        return op

    def finalize(self):
        keys = set()
        for e in self.ENG:
            c = {}
            for op in self.streams[e]:
                if op.is_dma:
                    op.sigval = 16 * op.seq
                    keys.add(op.key)
                elif op.signal:
                    c[op.key] = c.get(op.key, 0) + 1
                    op.sigval = c[op.key]
                    keys.add(op.key)
        return sorted(keys, key=str)


class Buf:
    def __init__(self, name, tens, F_el, base_dt, boff, dt, ncols):
        self.name, self.tens, self.boff, self.dt, self.ncols = name, tens, boff, dt, ncols
        self.esz = 4 if dt == F32 else 2
        bsz = 4 if base_dt == F32 else 2
        lo = boff // bsz
        hi = lo + ncols * self.esz // bsz
        v = tens[:, lo:hi]
        if dt != base_dt:
            v = v.bitcast(dt)
        self.v = v

    def pages(self, c0, n):
        b0 = self.boff + c0 * self.esz
        b1 = self.boff + (c0 + n) * self.esz
        return [(self.name, p) for p in range(b0 // PG, (b1 - 1) // PG + 1)]

    def ap(self, c0, n, p0=0, p1=128):
        return self.v[p0:p1, c0:c0 + n]

    def a3(self, c0, dims, p0=0, p1=128):
        base = self.v[p0:p1, c0:c0 + 1]
        pp = base.ap[0]
        return bass.AP(self.v.tensor, base.offset, [list(pp)] + [list(d) for d in dims])


DBG = {}


def build_nc(nseq, depth=DEPTH, stop_after=None):
    kvp = DBG.get("kv", "pkrv")
    nc = bass.Bass("TRN2", target_bir_lowering=False)
    NU = depth * UPL
    x_d = nc.dram_tensor("x", [nseq, SEQ, DM], F32, kind="ExternalInput")
    meta_d = nc.dram_tensor("meta", [NMETA, DM], F32, kind="ExternalInput")
    wu_d = nc.dram_tensor("wu", [NU, 128, 1024], F32, kind="ExternalInput")
    wv_d = nc.dram_tensor("wv", [depth, 128, 5120], F32, kind="ExternalInput")
    gains_d = nc.dram_tensor("gains", [depth, 128, 32], F32, kind="ExternalInput")
    sm_d = nc.dram_tensor("sm", [depth, 128, 8], F32, kind="ExternalInput")
    lam_d = nc.dram_tensor("lam", [depth, 256], F32, kind="ExternalInput")
    rb_d = nc.dram_tensor("rb", [32, 4], F32, kind="ExternalInput")
    cst_d = nc.dram_tensor("cst", [3, 128, 128], F32, kind="ExternalInput")
    ohr_d = nc.dram_tensor("ohr", [32, 2048], F32, kind="ExternalInput")
    rope_d = nc.dram_tensor("rope", [2, 128, L], F32, kind="ExternalInput")
    y_d = nc.dram_tensor("y", [nseq, SEQ, DM], F32, kind="ExternalOutput")
    wb_d = nc.dram_tensor("wb", [NU, 128, 1024], BF16)
    wvb_d = nc.dram_tensor("wvb", [depth, 128, 5120], BF16)
    bvd_d = nc.dram_tensor("bvd", [4, 2048], BF16)

    P = Prog()
    es = ExitStack()
    with es:
        def sb(name, cols, dt):
            return es.enter_context(nc.sbuf_tensor(name, [128, cols], dt))

        def mkbuf(name, cols, dt):
            t = sb(name, cols, dt)
            return Buf(name, t, cols, dt, 0, dt, cols)

        xT = mkbuf("xT", NCH * L, F32)
        KVa = sb("KVa", 26528, BF16)
        ARa = sb("ARa", 27648, BF16)
        HK = mkbuf("HK", 4 * HKW, BF16)
        RC = mkbuf("RC", L, BF16)
        RS = mkbuf("RS", L, BF16)
        NR = 6
        RING = mkbuf("RING", NR * 1024, BF16)
        LN = mkbuf("LN", 512, F32)
        LN2 = mkbuf("LN2", 512, F32)
        SQ = mkbuf("SQ", 2 * 512, BF16)
        IDN = mkbuf("IDN", 128, F32)
        JB = mkbuf("JB", 128, BF16)
        BD = mkbuf("BD", 128, BF16)
        ON1 = mkbuf("ON1", 128, BF16)
        GA = mkbuf("GA", depth * 32, F32)
        SM = mkbuf("SM", depth * 8, F32)
        LS = mkbuf("LS", depth * 8, F32)
        RBB = mkbuf("RBB", 128, F32)
        CZ = mkbuf("CZ", 4, F32)

        def sub(name, base, F_el, el_off, dt, ncols):
            return Buf(name, base, F_el, BF16, el_off * 2, dt, ncols)

        KA = sub("KVa", KVa, 26528, 0, BF16, 4 * L)
        KB = sub("KVa", KVa, 26528, 8256, BF16, 2 * L)
        VA = sub("KVa", KVa, 26528, 12384, BF16, 17 * 512)
        VB = sub("KVa", KVa, 26528, 21088, BF16, 17 * 320)
        UT = sub("KVa", KVa, 26528, 0, BF16, 32 * 512)
        XS = sub("KVa", KVa, 26528, 16384, F32, 2 * 1024)
        HTB = sub("ARa", ARa, 27648, 0, BF16, 8 * 512)
        YT = sub("ARa", ARa, 27648, 4096, F32, 8 * 512)
        QA = sub("ARa", ARa, 27648, 12288, BF16, 8 * 512)
        QB = sub("ARa", ARa, 27648, 16384, BF16, 8 * 512)
        OTB = sub("ARa", ARa, 27648, 20480, BF16, 8 * 512)
        PR = sub("ARa", ARa, 27648, 24576, BF16, 3 * 1024)
        A0 = sub("ARa", ARa, 27648, 4096 + 6 * 1024, F32, 512)
        A1 = sub("ARa", ARa, 27648, 4096 + 7 * 1024, F32, 512)
        RBT = sub("ARa", ARa, 27648, 4096 + 5 * 1024, F32, 512)
        WV = sub("ARa", ARa, 27648, 12288, BF16, 5120)
        HTB2 = sub("ARa", ARa, 27648, 17408, BF16, 8 * 512)
        STG = sub("ARa", ARa, 27648, 4096, F32, 4096)
        NPR = 3
        PSM = sub("ARa", ARa, 27648, 4096 + 7 * 1024, BF16, 1024)

        PSB = es.enter_context(nc.psum_tensor("psb", [128, 4096], F32))
        PS = [PSB[:, i * 512:(i + 1) * 512] for i in range(8)]


        def psr(i):
            return [("ps", i)]

        def dma(q, out, in_, reads, writes):
            P.add(q, lambda e: e.dma_start(out=out, in_=in_), reads, writes, dma=True)

        def mm(out, lhsT, rhs, start, stop, reads, writes):
            P.add("pe", lambda e: e.matmul(out, lhsT=lhsT, rhs=rhs, start=start, stop=stop), reads, writes)

        def act(out, in_, func, reads, writes, bias=None, scale=1.0):
            if func == AF.Copy:
                P.add("act", lambda e: e.activation(out=out, in_=in_, func=func), reads, writes)
                return
            if bias is None:
                bias = CZ.ap(0, 1, 0, out.shape[0])
            P.add("act", lambda e: e.activation(out=out, in_=in_, func=func, bias=bias, scale=scale),
                  list(reads) + CZ.pages(0, 4), writes)

        def tt(eng, out, in0, in1, op, reads, writes):
            P.add(eng, lambda e: e.tensor_tensor(out=out, in0=in0, in1=in1, op=op), reads, writes)

        def stt(eng, out, in0, scalar, in1, op0, op1, reads, writes):
            P.add(eng, lambda e: e.scalar_tensor_tensor(out=out, in0=in0, scalar=scalar, in1=in1, op0=op0, op1=op1),
                  reads, writes)

        def tsc(eng, out, in0, s1, op0, reads, writes, s2=None, op1=None):
            if op1 is None:
                P.add(eng, lambda e: e.tensor_scalar(out=out, in0=in0, scalar1=s1, scalar2=None, op0=op0), reads, writes)
            else:
                P.add(eng, lambda e: e.tensor_scalar(out=out, in0=in0, scalar1=s1, scalar2=s2, op0=op0, op1=op1),
                      reads, writes)

        def cpy(eng, out, in_, reads, writes):
            P.add(eng, lambda e: e.tensor_copy(out=out, in_=in_), reads, writes)

        def memset(eng, out, val, writes):
            P.add(eng, lambda e: e.memset(out, val), (), writes)

        memset("dve", CZ.ap(0, 1), 0.0, CZ.pages(0, 4))
        memset("dve", CZ.ap(1, 1), EPS, CZ.pages(0, 4))
        memset("dve", ON1.ap(0, 128), 1.0, ON1.pages(0, 128))
        dma("sp", IDN.ap(0, 128), cst_d[0], [("cst",)], IDN.pages(0, 128))
        dma("sp", STG.ap(0, 128), cst_d[1], [("cst",)], STG.pages(0, 128))
        dma("sp", STG.ap(128, 128), cst_d[2], [("cst",)], STG.pages(128, 128))
        cpy("dve", JB.ap(0, 128), STG.ap(0, 128), STG.pages(0, 128), JB.pages(0, 128))
        cpy("dve", BD.ap(0, 128), STG.ap(128, 128), STG.pages(128, 128), BD.pages(0, 128))
        for l in range(depth):
            dma("sp", GA.ap(l * 32, 32), gains_d[l], [("gains",)], GA.pages(l * 32, 32))
            dma("sp", SM.ap(l * 8, 8), sm_d[l], [("sm",)], SM.pages(l * 8, 8))
        dma("sp", RBB.ap(0, 128), bass.AP(rb_d, 0, [[0, 128], [1, 128]]), [("rb",)], RBB.pages(0, 128))
        for i, RT in enumerate((RC, RS)):
            for c0 in range(0, L, 1032):
                dma("sp", STG.ap(1024, 1032), rope_d[i, :, c0:c0 + 1032], [("rope",)], STG.pages(1024, 1032))
                cpy("dve", RT.ap(c0, 1032), STG.ap(1024, 1032), STG.pages(1024, 1032), RT.pages(c0, 1032))
        for l in range(depth):
            lam_init = 0.8 - 0.6 * math.exp(-0.3 * l)
            dma("sp", STG.ap(2560, 256), bass.AP(lam_d, l * 256, [[0, 128], [1, 256]]), [("lam",)], STG.pages(2560, 256))
            tt("dve", STG.ap(2816, 64), STG.ap(2560, 64), STG.ap(2624, 64), ALU.mult, STG.pages(2560, 256), STG.pages(2816, 128))
            tt("dve", STG.ap(2880, 64), STG.ap(2688, 64), STG.ap(2752, 64), ALU.mult, STG.pages(2560, 256), STG.pages(2816, 128))
            P.add("dve", lambda e, o=LS.ap(l * 8 + 2, 1), i=STG.ap(2816, 64): e.reduce_sum(out=o, in_=i, axis=AX.X),
                  STG.pages(2816, 128), LS.pages(0, depth * 8))
            P.add("dve", lambda e, o=LS.ap(l * 8 + 3, 1), i=STG.ap(2880, 64): e.reduce_sum(out=o, in_=i, axis=AX.X),
                  STG.pages(2816, 128), LS.pages(0, depth * 8))
            act(LS.ap(l * 8 + 4, 2), LS.ap(l * 8 + 2, 2), AF.Exp, LS.pages(0, depth * 8), LS.pages(0, depth * 8))
            tt("dve", LS.ap(l * 8 + 0, 1), LS.ap(l * 8 + 5, 1), LS.ap(l * 8 + 4, 1), ALU.subtract,
               LS.pages(0, depth * 8), LS.pages(0, depth * 8))
            tsc("dve", LS.ap(l * 8 + 0, 1), LS.ap(l * 8 + 0, 1), -lam_init, ALU.add, LS.pages(0, depth * 8), LS.pages(0, depth * 8))
            tsc("dve", LS.ap(l * 8 + 1, 1), SM.ap(l * 8 + 0, 1), 1.0 - lam_init, ALU.mult,
                SM.pages(0, depth * 8) + LS.pages(0, depth * 8), LS.pages(0, depth * 8))
        dma("sp", STG.ap(3072, 4, 0, 32), rb_d[:, :], [("rb",)], STG.pages(3072, 4))
        for c in range(4):
            dma("sp", STG.ap(1024, 512, 0, 32), ohr_d[:, c * 512:(c + 1) * 512], [("ohr",)], STG.pages(1024, 512))
            mm(PS[c][0:4, :], STG.ap(3072, 4, 0, 32), STG.ap(1024, 512, 0, 32), True, True,
               STG.pages(3072, 4) + STG.pages(1024, 512), psr(c))
            cpy("dve", SQ.ap(0, 512, 0, 4), PS[c][0:4, :], psr(c), SQ.pages(0, 512))
            dma("sp", bvd_d[:, c * 512:(c + 1) * 512], SQ.ap(0, 512, 0, 4), SQ.pages(0, 512), [("bvd",)])
        band_offs = []
        mixed_const = set()
        for (qc, nq) in QBLK:
            segs = q_segments(qc, nq)
            for (kc, nk) in KTILES:
                sps = [tile_bias_spec(kc, nk, qc + so, sn) for so, sn in segs]
                uniform = all(sp[0] == "const" for sp in sps) and len(set(sp[1] for sp in sps)) == 1
                for (so, sn), sp_ in zip(segs, sps):
                    if sp_[0] == "band":
                        band_offs.append((sp_[1], sp_[1] + sn))
                    elif not uniform:
                        mixed_const.add((sp_[1], nk, sn))
        HKBASE = min(a for a, _ in band_offs)
        assert max(b for _, b in band_offs) - HKBASE <= HKW and HKBASE >= BVR_LO
        assert HKBASE + 127 + HKW <= BVR_LO + 2048
        const_off = {}
        for (bk_, nk_, sn_) in mixed_const:
            found = None
            for off in list(range(0, HKW - sn_ + 1)):
                lo = HKBASE + off
                hi = HKBASE + off + nk_ - 1 + sn_ - 1
                rels = (L - 1) - np.arange(lo, hi + 1)
                if (t5_bucket(rels) == bk_).all():
                    found = off
                    break
            assert found is not None, (bk_, nk_, sn_)
            const_off[(bk_, nk_, sn_)] = found
        for h in range(4):
            dma("sp", HK.ap(h * HKW, HKW),
                bass.AP(bvd_d, h * 2048 + HKBASE - BVR_LO, [[1, 128], [1, HKW]]), [("bvd",)], HK.pages(h * HKW, HKW))

        cast_engs = ["dve", "pool", "act"]
        conv_n = [0]
        YTB = [sub("ARa", ARa, 27648, 4096 + 4 * 1024, BF16, 1024) for j in range(2)]

        def conv(item, late):
            kind, a, b = item
            src = wu_d[a] if kind == "u" else wv_d[a, :, b * 1024:(b + 1) * 1024]
            dstd = wb_d[a] if kind == "u" else wvb_d[a, :, b * 1024:(b + 1) * 1024]
            res = [("wb", a)] if kind == "u" else [("wvb", a)]
            i = conv_n[0]
            conv_n[0] += 1
            if not late:
                st_, stp = STG.ap((i % 4) * 1024, 1024), STG.pages((i % 4) * 1024, 1024)
                dst, dpg = RING.ap((i % NR) * 1024, 1024), RING.pages((i % NR) * 1024, 1024)
                dma("sp", st_, src, [("wsrc",)], stp)
                ce = cast_engs[i % 3]
                if ce == "act":
                    act(dst, st_, AF.Copy, stp, dpg)
                else:
                    cpy(ce, dst, st_, stp, dpg)
                dma("sp", dstd, dst, dpg, res)
            else:
                j = i % 2
                st_, stp = YT.ap(j * 1024, 1024), YT.pages(j * 1024, 1024)
                dma("pool", st_, src, [("wsrc",)], stp)
                conv_flush()
                late_pend.append((j, dstd, res))

        late_pend = []

        def conv_flush():
            if late_pend:
                j, dstd, res = late_pend.pop()
                st_, stp = YT.ap(j * 1024, 1024), YT.pages(j * 1024, 1024)
                dst, dpg = YTB[j].ap(0, 1024), YTB[j].pages(0, 1024)
                cpy("pool", dst, st_, stp, dpg)
                dma("pool", dstd, dst, dpg, res)

        early = [("u", u, 0) for u in range(min(28, NU))] + [("v", 0, k) for k in range(5)]
        late_items = [("u", u, 0) for u in range(28, NU)] + [("v", l, k) for l in range(1, depth) for k in range(5)]
        if not DBG.get("lateconv", 1):
            early, late_items = early + late_items, []
        for it in early:
            conv(it, False)
        for t in range(17):
            memset("pool", VB.ap(t * 320 + 64, 64), 1.0, VB.pages(t * 320 + 64, 64))
            memset("pool", VB.ap(t * 320 + 192, 64), 1.0, VB.pages(t * 320 + 192, 64))

        ring_ctr = [0]

        def load_unit(u):
            i = ring_ctr[0] % NR
            ring_ctr[0] += 1
            dma("sp", RING.ap(i * 1024, 1024), wb_d[u], [("wb", u)], RING.pages(i * 1024, 1024))
            return i

        def ring_w(i, j):
            return RING.ap(i * 1024 + j * 128, 128), RING.pages(i * 1024 + j * 128, 128)

        bank_ctr = [0]

        def next_bank(lo=0, hi=7):
            b = lo + bank_ctr[0] % (hi - lo)
            bank_ctr[0] += 1
            return b

        sq_ctr = [0]

        def xt(c, c0, n):
            return xT.ap(c * L + c0, n), xT.pages(c * L + c0, n)

        def stats_finish(n, scale_ln=0.0, LNt=None, bank=7, ms_scale=1.0 / 1024.0):
            if LNt is None:
                LNt = (LN, 0)
            lb, lc = LNt
            la, lp = lb.ap(lc, n), lb.pages(lc, 512)
            act(la, PS[bank][:, 0:n], AF.Ln, psr(bank), lp, bias=CZ.ap(1, 1), scale=ms_scale)
            if scale_ln == 0.0:
                act(la, la, AF.Exp, lp, lp, scale=-0.5)
            else:
                act(la, la, AF.Exp, lp, lp, bias=CZ.ap(2, 1), scale=-0.5)

        def prenorm(c0, n, gcol, l, HT, bank=7):
            for c in range(NCH):
                i = sq_ctr[0] % 2
                sq_ctr[0] += 1
                xa, xp = xt(c, c0, n)
                tt("pool", SQ.ap(i * 512, n), xa, xa, ALU.mult, xp, SQ.pages(i * 512, 512))
                mm(PS[bank][:, 0:n], ON1.ap(0, 128), SQ.ap(i * 512, n), c == 0, c == NCH - 1,
                   SQ.pages(i * 512, 512) + ON1.pages(0, 128), psr(bank))
            stats_finish(n, LNt=(LN2, 0), bank=bank)
            for c in range(NCH):
                xa, xp = xt(c, c0, n)
                stt("dve", HT.ap(c * 512, n), xa, GA.ap(l * 32 + gcol * 8 + c, 1), LN2.ap(0, n),
                    ALU.mult, ALU.mult, xp + LN2.pages(0, 512) + GA.pages(0, depth * 32), HT.pages(c * 512, n))

        def prenorm_staged(c0, n, gcol, l, HT, deferred, j0):
            def sqmm(c):
                def f():
                    i = sq_ctr[0] % 2
                    sq_ctr[0] += 1
                    xa, xp = xt(c, c0, n)
                    tt("pool", SQ.ap(i * 512, n), xa, xa, ALU.mult, xp, SQ.pages(i * 512, 512))
                    mm(PS[7][:, 0:n], ON1.ap(0, 128), SQ.ap(i * 512, n), c == 0, c == NCH - 1,
                       SQ.pages(i * 512, 512) + ON1.pages(0, 128), psr(7))
                return f
            for c in range(NCH):
                deferred.append((j0 + c, sqmm(c)))
            deferred.append((j0 + NCH + 1, lambda: stats_finish(n, LNt=(LN2, 0), bank=7)))

            def applies():
                for c in range(NCH):
                    xa, xp = xt(c, c0, n)
                    stt("dve", HT.ap(c * 512, n), xa, GA.ap(l * 32 + gcol * 8 + c, 1), LN2.ap(0, n),
                        ALU.mult, ALU.mult, xp + LN2.pages(0, 512) + GA.pages(0, depth * 32), HT.pages(c * 512, n))
            deferred.append((j0 + NCH + 3, applies))

        def proj_postnorm(src, nk_chunks, units, c0, n, gcol, l, bhi=7):
            pend = []

            def stat_mm(oc, i):
                mm(PS[7][:, 0:n], ON1.ap(0, 128), SQ.ap(i * 512, n), oc == 0, oc == NCH - 1,
                   SQ.pages(i * 512, 512) + ON1.pages(0, 128), psr(7))

            for oc in range(NCH):
                b = next_bank(0, bhi)
                nkk = len(units[oc]) * 8
                kk = 0
                for u in units[oc]:
                    ri = load_unit(u)
                    for j in range(8):
                        wa, wp = ring_w(ri, j)
                        mm(PS[b][:, 0:n], wa, src.ap(kk * 512, n), kk == 0, kk == nkk - 1,
                           wp + src.pages(kk * 512, n), psr(b))
                        kk += 1
                if pend:
                    stat_mm(*pend.pop())
                cpy("dve", YT.ap(oc * 512, n), PS[b][:, 0:n], psr(b), YT.pages(oc * 512, n))
                i = sq_ctr[0] % 2
                sq_ctr[0] += 1
                tt("pool", SQ.ap(i * 512, n), YT.ap(oc * 512, n), YT.ap(oc * 512, n), ALU.mult, YT.pages(oc * 512, n),
                   SQ.pages(i * 512, 512))
                pend.append((oc, i))
            stat_mm(*pend.pop())
            stats_finish(n)
            for oc in range(NCH):
                e = "pool"
                stt("dve", YT.ap(oc * 512, n), YT.ap(oc * 512, n), GA.ap(l * 32 + gcol * 8 + oc, 1), LN.ap(0, n),
                    ALU.mult, ALU.mult, YT.pages(oc * 512, n) + LN.pages(0, 512), YT.pages(oc * 512, n))
                xa, xp = xt(oc, c0, n)
                tt(e, xa, xa, YT.ap(oc * 512, n), ALU.add, xp + YT.pages(oc * 512, n), xp)

        rope_ctr = [0]

        def rope_chain(bx, by, n, c0, gcol, l, dst, dpg, is_q, split=None):
            k = rope_ctr[0] % 2
            rope_ctr[0] += 1
            t0, t1, tl = 3 * k, 3 * k + 1, 3 * k + 2
            T0, T0p = YT.ap(t0 * 512, n), YT.pages(t0 * 512, 512)
            T1, T1p = YT.ap(t1 * 512, n), YT.pages(t1 * 512, 512)
            TL, TLp = YT.ap(tl * 512, n), YT.pages(tl * 512, 512)
            i = sq_ctr[0] % 2
            sq_ctr[0] += 1
            act(SQ.ap(i * 512, n), PS[bx][:, 0:n], AF.Square, psr(bx), SQ.pages(i * 512, 512))
            mm(PS[7][:, 0:n], BD.ap(0, 128), SQ.ap(i * 512, n), True, True, SQ.pages(i * 512, 512) + BD.pages(0, 128), psr(7))
            stats_finish(n, scale_ln=(1.0 if is_q else 0.0), LNt=(YT, tl * 512), ms_scale=1.0)
            stt("dve", T0, PS[bx][:, 0:n], SM.ap(l * 8 + gcol, 1), RC.ap(c0, n), ALU.mult, ALU.mult,
                psr(bx) + RC.pages(c0, n) + SM.pages(0, depth * 8), T0p)
            stt("dve", T1, PS[by][:, 0:n], SM.ap(l * 8 + gcol + 1, 1), RS.ap(c0, n), ALU.mult, ALU.mult,
                psr(by) + RS.pages(c0, n) + SM.pages(0, depth * 8), T1p)
            tt("pool", T0, T0, T1, ALU.add, T0p + T1p, T0p)
            if split is None:
                tt("pool", dst, T0, TL, ALU.mult, T0p + TLp, dpg)
            else:
                for (dap, dpg_, r0_) in split:
                    tt("pool", dap, YT.ap(t0 * 512, n, r0_, r0_ + 64), YT.ap(tl * 512, n, r0_, r0_ + 64), ALU.mult,
                       T0p + TLp, dpg_)

        memset("dve", CZ.ap(2, 1), math.log(0.125), CZ.pages(0, 4))

        def proj_chunk(HT, u, n):
            ri = load_unit(u)
            b = next_bank()
            for j in range(8):
                wa, wp = ring_w(ri, j)
                mm(PS[b][:, 0:n], wa, HT.ap(j * 512, n), j == 0, j == 7, wp + HT.pages(j * 512, n), psr(b))
            return b

        stages = {None: 99, "in": 0, "kv": 1, "attn": 2, "l0": 3}
        stop_lvl = stages[stop_after]
        tr_ctr = [0]
        for s in range(nseq):
            for t in range(17):
                k = tr_ctr[0] % 2
                tr_ctr[0] += 1
                nk = 128 if t < 16 else 16
                src = x_d[s, t * 128:(t + 1) * 128, :] if t < 16 else meta_d[:, :]
                dma("act", XS.ap(k * 1024, 1024, 0, nk), src, [("x", s)], XS.pages(k * 1024, 1024))
                for half in range(2):
                    b = next_bank()
                    for q in range(4):
                        c = half * 4 + q
                        P.add("pe", lambda e, o=PS[b][:, q * 128:q * 128 + nk], i=XS.ap(k * 1024 + c * 128, 128, 0, nk),
                              idn=IDN.ap(0, nk, 0, nk): e.transpose(o, i, idn),
                              XS.pages(k * 1024 + c * 128, 128) + IDN.pages(0, 128), psr(b))
                    o3 = xT.a3((half * 4) * L + t * 128, [[L, 4], [1, nk]])
                    i3 = bass.AP(PSB, b * 512, [[4096, 128], [128, 4], [1, nk]])
                    pg = []
                    for q in range(4):
                        pg += xT.pages((half * 4 + q) * L + t * 128, nk)
                    cpy("dve" if half == 0 else "pool" if False else "dve", o3, i3, psr(b), pg)
            for l in range(depth):
                if stop_lvl < 1:
                    break
                P.new_epoch()
                ub = l * UPL
                dma("sp", WV.ap(0, 5120), wvb_d[l], [("wvb", l)], WV.pages(0, 5120))
                prenorm(BLOCKS[0][0], BLOCKS[0][1], 0, l, HTB)
                for bi, (c0, n) in enumerate(BLOCKS):
                    HT = HTB if bi % 2 == 0 else HTB2
                    kvdef = []
                    if bi + 1 < len(BLOCKS):
                        prenorm_staged(BLOCKS[bi + 1][0], BLOCKS[bi + 1][1], 0, l, HTB2 if bi % 2 == 0 else HTB, kvdef, 0)
                    ntile = (n + 127) // 128
                    per = -(-len(kvdef) // ntile) if kvdef else 0
                    for h in range(4 if "k" in kvp else 0):
                        b = proj_chunk(HT, ub + U_AK + h, n)
                        cpy("dve", KA.ap(h * L + c0, n), PS[b][:, 0:n], psr(b), KA.pages(h * L + c0, n))
                    for g in range(2 if "r" in kvp else 0):
                        bx = proj_chunk(HT, ub + U_BK + g, n)
                        by = proj_chunk(HT, ub + U_BKS + g, n)
                        rope_chain(bx, by, n, c0, 3, l, KB.ap(g * L + c0, n), KB.pages(g * L + c0, n), False)
                    for tc in range(c0, c0 + n if "v" in kvp else c0, 128):
                        t = tc // 128
                        nk = min(128, c0 + n - tc)
                        b = next_bank()
                        for j in range(8):
                            mm(PS[b][0:nk, :], HT.ap(j * 512 + tc - c0, nk), WV.ap(j * 640, 512), j == 0, j == 7,
                               HT.pages(j * 512 + tc - c0, nk) + WV.pages(j * 640, 512), psr(b))
                        cpy("dve", VA.ap(t * 512, 512, 0, nk), PS[b][0:nk, :], psr(b), VA.pages(t * 512, 512))
                        b = next_bank()
                        for j in range(8):
                            mm(PS[b][0:nk, 0:128], HT.ap(j * 512 + tc - c0, nk), WV.ap(j * 640 + 512, 128), j == 0, j == 7,
                               HT.pages(j * 512 + tc - c0, nk) + WV.pages(j * 640 + 512, 128), psr(b))
                        o3 = VB.a3(t * 320, [[128, 2], [1, 64]], 0, nk)
                        i3 = bass.AP(PSB, b * 512, [[4096, nk], [64, 2], [1, 64]])
                        cpy("dve", o3, i3, psr(b), VB.pages(t * 320, 64) + VB.pages(t * 320 + 128, 64))
                        cpy("dve", VB.ap(t * 320 + 256, 64, 0, nk), PS[b][0:nk, 0:64], psr(b), VB.pages(t * 320 + 256, 64))
                        for _ in range(per):
                            if kvdef:
                                kvdef.pop(0)[1]()
                    while kvdef:
                        kvdef.pop(0)[1]()
                if stop_lvl < 2:
                    continue
                pr_ctr = [0]
                acc_ctr = [0]
                s_ctr = [0]
                def qproj_a(c0, n):
                    for h in range(4):
                        b = proj_chunk(HTB, ub + U_AQ + h, n)
                        for m_ in range(2):
                            r0_ = 64 * m_
                            tsc("dve", QA.ap((2 * h + m_) * 512, n, r0_, r0_ + 64), PS[b][r0_:r0_ + 64, 0:n], 0.125, ALU.mult,
                                psr(b), QA.pages((2 * h + m_) * 512, n))

                def qproj_b(c0, n):
                    for c in range(4):
                        bx = proj_chunk(HTB, ub + U_BQ + c, n)
                        by = proj_chunk(HTB, ub + U_BQS + c, n)
                        rope_chain(bx, by, n, c0, 1, l, None, None, True,
                                   split=[(QB.ap((2 * c) * 512, n, 0, 64), QB.pages((2 * c) * 512, n), 0),
                                          (QB.ap((2 * c + 1) * 512, n, 64, 128), QB.pages((2 * c + 1) * 512, n), 64)])

                memset("pool", QA.ap(0, 4096), 0.0, QA.pages(0, 4096))
                memset("pool", QB.ap(0, 4096), 0.0, QB.pages(0, 4096))
                prenorm(QBLK[0][0], QBLK[0][1], 0, l, HTB)
                qproj_a(*QBLK[0])
                qproj_b(*QBLK[0])
                for bi, (c0, n) in enumerate(QBLK):
                    groups = []
                    for i in range(8):
                        groups.append(("A", i // 2, i % 2))
                        groups.append(("B", i, 0))
                    kpairs = [[KTILES[2 * i], KTILES[2 * i + 1]] for i in range(8)] + [[KTILES[16]]]
                    jobs = []
                    for gi, g in enumerate(groups):
                        for pi_, kp in enumerate(kpairs):
                            jobs.append((gi, g, pi_, kp))
                    LA = 2

                    def emit_qk(job):
                        gi, g, pj, kp = job
                        kind, h, m = g
                        sl = s_ctr[0] % 2
                        s_ctr[0] += 1
                        pi = pr_ctr[0] % NPR
                        pr_ctr[0] += 1
                        specs = []
                        segs = q_segments(c0, n)
                        for ti, (kc, nk) in enumerate(kp):
                            sb_ = 2 * sl + ti
                            if kind == "A":
                                r0 = 0
                                kap, kpg = KA.ap(h * L + kc, nk), KA.pages(h * L + kc, nk)
                                qb_, qoff = QA, (2 * h + m) * 512
                                sps = [tile_bias_spec(kc, nk, c0 + so, sn) for so, sn in segs]
                                uniform = all(sp[0] == "const" for sp in sps) and len(set(sp[1] for sp in sps)) == 1
                                spec = sps[0] if uniform else ("band", 0)
                            else:
                                r0 = 0
                                gk = h // 4
                                kap, kpg = KB.ap(gk * L + kc, nk), KB.pages(gk * L + kc, nk)
                                qb_, qoff = QB, h * 512
                                sps = None
                                spec = ("none", 0)
                            specs.append(spec)
                            if spec[0] != "band":
                                mm(PS[sb_][0:nk, 0:n], kap, qb_.ap(qoff, n), True, True,
                                   kpg + qb_.pages(qoff, n), psr(sb_))
                            else:
                                for (so, sn), sp_ in zip(segs, sps):
                                    mm(PS[sb_][0:nk, so:so + sn], kap, qb_.ap(qoff + so, sn), True, False,
                                       kpg + qb_.pages(qoff + so, sn), psr(sb_))
                                    off = (sp_[1] - HKBASE) if sp_[0] == "band" else const_off[(sp_[1], nk, sn)]
                                    mm(PS[sb_][0:nk, so:so + sn], JB.ap(128 - nk, nk), HK.ap(h * HKW + off, sn), False, True,
                                       JB.pages(0, 128) + HK.pages(h * HKW + off, sn), psr(sb_))

                        def bias_of(spec, nk):
                            if spec[0] == "const":
                                return RBB.ap(spec[1] * 4 + h, 1, 0, nk)
                            return CZ.ap(0, 1, 0, nk)

                        def bkey(spec):
                            return spec[1] if spec[0] == "const" else -1

                        if len(kp) == 2 and bkey(specs[0]) == bkey(specs[1]):
                            src = bass.AP(PSB, 2 * sl * 512, [[4096, 128], [512, 2], [1, n]])
                            dst = PR.a3(pi * 1024, [[512, 2], [1, n]])
                            act(dst, src, AF.Exp, psr(2 * sl) + psr(2 * sl + 1) + RBB.pages(0, 128),
                                PR.pages(pi * 1024, 1024), bias=bias_of(specs[0], 128))
                        else:
                            for ti, (kc, nk) in enumerate(kp):
                                sb_ = 2 * sl + ti
                                act(PR.ap(pi * 1024 + ti * 512, n, 0, nk), PS[sb_][0:nk, 0:n], AF.Exp,
                                    psr(sb_) + RBB.pages(0, 128), PR.pages(pi * 1024 + ti * 512, 512),
                                    bias=bias_of(specs[ti], nk))
                        return pi

                    def emit_pv(job, pi):
                        gi, g, pj, kp = job
                        kind, h, m = g
                        for ti, (kc, nk) in enumerate(kp):
                            t = kc // 128
                            first = pj == 0 and ti == 0
                            last = pj == len(kpairs) - 1 and ti == len(kp) - 1
                            pap, ppg = PR.ap(pi * 1024 + ti * 512, n, 0, nk), PR.pages(pi * 1024 + ti * 512, 512)
                            if kind == "A":
                                bo, bs = 4, 5
                                p2 = DBG.get("p2", 0)
                                if p2 and len(kp) == 2 and ti == 0:
                                    ps_ = psm_ctr[0] % 2
                                    psm_ctr[0] += 1
                                    tt(DBG.get("p2eng", "pool"), PSM.ap(ps_ * 512, n), PR.ap(pi * 1024, n), PR.ap(pi * 1024 + 512, n), ALU.add,
                                       PR.pages(pi * 1024, 1024), PSM.pages(ps_ * 512, 512))
                                    if pend_ones:
                                        pend_ones.pop()()

                                    def ones_mm(ps_=ps_, st_=(pj == 0)):
                                        mm(PS[bs][:, 0:n], ON1.ap(0, 128), PSM.ap(ps_ * 512, n), st_, False,
                                           ON1.pages(0, 128) + PSM.pages(ps_ * 512, 512), psr(bs))
                                    pend_ones.append(ones_mm)
                                if p2 and len(kp) == 1 and pend_ones:
                                    pend_ones.pop()()
                                mm(PS[bo][:, 0:n], VA.ap(t * 512 + h * 128, 128, 0, nk), pap, first, last,
                                   VA.pages(t * 512 + h * 128, 128) + ppg, psr(bo))
                                if not p2:
                                    mm(PS[bs][:, 0:n], ON1.ap(0, 128, 0, nk), pap, first, last, ON1.pages(0, 128) + ppg, psr(bs))
                                elif len(kp) == 1:
                                    mm(PS[bs][:, 0:n], ON1.ap(0, 128, 0, nk), pap, False, last, ON1.pages(0, 128) + ppg, psr(bs))
                            else:
                                bo = 6
                                gk = h // 4
                                odd = h % 2
                                if odd == 0:
                                    vc = 0 if gk == 0 else 128
                                else:
                                    vc = 192 if gk == 0 else 64
                                mm(PS[bo][:, 0:n], VB.ap(t * 320 + vc, 128, 0, nk), pap, first, last,
                                   VB.pages(t * 320 + vc, 128) + ppg, psr(bo))
                        if pj != len(kpairs) - 1:
                            return
                        if kind == "A":
                            bo, bs = 4, 5
                            AT = A0 if m == 0 else A1
                            P.add("dve", lambda e, o=AT.ap(0, n), i=PS[bs][:, 0:n]: (e.reciprocal_approx_fast(out=o, in_=i) if DBG.get('rfast', 0) else e.reciprocal(out=o, in_=i)),
                                  psr(bs), AT.pages(0, 512))
                            tt("dve", AT.ap(0, n), PS[bo][:, 0:n], AT.ap(0, n), ALU.mult, psr(bo) + AT.pages(0, 512),
                               AT.pages(0, 512))
                            if m == 1:
                                def stage2(h=h):
                                    stt("dve", A0.ap(0, n), A1.ap(0, n), LS.ap(l * 8 + 0, 1), A0.ap(0, n), ALU.mult, ALU.add,
                                        A0.pages(0, 512) + A1.pages(0, 512) + LS.pages(0, depth * 8), A0.pages(0, 512))
                                    i = sq_ctr[0] % 2
                                    sq_ctr[0] += 1
                                    tt("pool", SQ.ap(i * 512, n), A0.ap(0, n), A0.ap(0, n), ALU.mult, A0.pages(0, 512),
                                       SQ.pages(i * 512, 512))
                                    sq_slot[0] = i

                                def stage2b(h=h):
                                    i = sq_slot[0]
                                    mm(PS[7][:, 0:n], ON1.ap(0, 128), SQ.ap(i * 512, n), True, True,
                                       SQ.pages(i * 512, 512) + ON1.pages(0, 128), psr(7))

                                def stage3(h=h):
                                    stats_finish(n, ms_scale=1.0 / 128.0)

                                def stage4(h=h):
                                    stt("dve", OTB.ap(h * 512, n), A0.ap(0, n), LS.ap(l * 8 + 1, 1), LN.ap(0, n), ALU.mult, ALU.mult,
                                        A0.pages(0, 512) + LN.pages(0, 512) + LS.pages(0, depth * 8), OTB.pages(h * 512, n))
                                deferred.append((cur_step[0] + 1, stage2))
                                deferred.append((cur_step[0] + 6, stage2b))
                                deferred.append((cur_step[0] + 8, stage3))
                                deferred.append((cur_step[0] + 10, stage4))
                        else:
                            bo = 6
                            odd = h % 2
                            orow, srow = (0, 64) if odd == 0 else (64, 0)
                            P.add("dve", lambda e, o=RBT.ap(0, n, srow, srow + 64), i=PS[bo][srow:srow + 64, 0:n]:
                                  (e.reciprocal_approx_fast(out=o, in_=i) if DBG.get('rfast', 0) else e.reciprocal(out=o, in_=i)), psr(bo), RBT.pages(0, 512))
                            tt("dve", OTB.ap((4 + h // 2) * 512, n, orow, orow + 64), PS[bo][orow:orow + 64, 0:n],
                               RBT.ap(0, n, srow, srow + 64), ALU.mult, psr(bo) + RBT.pages(0, 512),
                               OTB.pages((4 + h // 2) * 512, n))

                    deferred = []
                    cur_step = [0]
                    pend_ones = []
                    sq_slot = [0]
                    psm_ctr = [0]
                    if bi + 1 < len(QBLK) and DBG.get("p1", 1):
                        prenorm_staged(QBLK[bi + 1][0], QBLK[bi + 1][1], 0, l, HTB, deferred, 38)

                    def run_deferred(upto):
                        keep = []
                        for due, fn in deferred:
                            if due <= upto:
                                fn()
                            else:
                                keep.append((due, fn))
                        deferred[:] = keep

                    pend = []
                    for st in range(len(jobs) + LA):
                        cur_step[0] = st
                        run_deferred(st)
                        if late_items and st % 4 == 1:
                            conv(late_items.pop(0), True)
                        if st < len(jobs):
                            pend.append(emit_qk(jobs[st]))
                        if st >= LA:
                            emit_pv(jobs[st - LA], pend[st - LA])
                    conv_flush()
                    if bi + 1 < len(QBLK):
                        run_deferred(len(jobs) + LA)
                        qproj_a(*QBLK[bi + 1])
                    run_deferred(10 ** 9)
                    proj_postnorm(OTB, 8, [[ub + U_WO + oc] for oc in range(NCH)], c0, n, 1, l)
                    if bi + 1 < len(QBLK):
                        qproj_b(*QBLK[bi + 1])
                if stop_lvl < 3:
                    continue
                while late_items:
                    conv(late_items.pop(0), True)
                conv_flush()
                MB = [(0, 416), (416, 416), (832, 416), (1248, 416), (1664, 400)]
                prenorm(MB[0][0], MB[0][1], 2, l, HTB, bank=6)
                for bi, (c0, n) in enumerate(MB):
                    HT = HTB if bi % 2 == 0 else HTB2
                    for f in range(32):
                        ri = load_unit(ub + U_UP + f)
                        b = next_bank(0, 6)
                        for j in range(8):
                            wa, wp = ring_w(ri, j)
                            mm(PS[b][:, 0:n], wa, HT.ap(j * 512, n), j == 0, j == 7, wp + HT.pages(j * 512, n), psr(b))
                        yi = f % 8
                        act(YT.ap(yi * 512, n), PS[b][:, 0:n], AF.Relu, psr(b), YT.pages(yi * 512, n))
                        tt("pool", UT.ap(f * 512, n), YT.ap(yi * 512, n), YT.ap(yi * 512, n), ALU.mult,
                           YT.pages(yi * 512, n), UT.pages(f * 512, n))
                    if bi + 1 < len(MB):
                        prenorm(MB[bi + 1][0], MB[bi + 1][1], 2, l, HTB2 if bi % 2 == 0 else HTB, bank=6)
                    proj_postnorm(UT, 32, [[ub + U_DN + oc * 4 + kg for kg in range(4)] for oc in range(NCH)], c0, n, 3, l, bhi=6)
                if stop_lvl == 3:
                    break
            for t in range(16):
                k = tr_ctr[0] % 2
                tr_ctr[0] += 1
                for half in range(2):
                    b = next_bank()
                    for q in range(4):
                        c = half * 4 + q
                        xa, xp = xt(c, t * 128, 128)
                        P.add("pe", lambda e, o=PS[b][:, q * 128:(q + 1) * 128], i=xa, idn=IDN.ap(0, 128): e.transpose(o, i, idn),
                              xp + IDN.pages(0, 128), psr(b))
                    cpy("dve", XS.ap(k * 1024 + half * 512, 512), PS[b][:, :], psr(b), XS.pages(k * 1024 + half * 512, 512))
                dma("act", y_d[s, t * 128:(t + 1) * 128, :], XS.ap(k * 1024, 1024), XS.pages(k * 1024, 1024), [("y", s, t)])

        keys = P.finalize()
        sems = {k: es.enter_context(nc.semaphore(f"s{i}")) for i, k in enumerate(keys)}
        final = {}
        for q in P.DMAK:
            for op in P.dma_hist[q]:
                final[op.key] = (q, op.sigval)
        blk = es.enter_context(nc.Block())

        def emit(e, name):
            for op in P.streams[name]:
                for d in op.waits:
                    e.wait_ge(sems[d.key], d.sigval)
                ins = op.fn(e)
                if op.signal:
                    ins.then_inc(sems[op.key], 16 if op.is_dma else 1)
            for k, (q, v) in final.items():
                if q == name:
                    e.wait_ge(sems[k], v)

        @blk.tensor
        def _(e):
            emit(e, "pe")

        @blk.scalar
        def _(e):
            emit(e, "act")

        @blk.vector
        def _(e):
            emit(e, "dve")

        @blk.gpsimd
        def _(e):
            emit(e, "pool")

        @blk.sync
        def _(e):
            emit(e, "sp")
    nc._prog_stats = {e: len(P.streams[e]) for e in P.ENG}
    nc._nsem = len(keys)
    return nc


def _unit(wblk):
    return np.ascontiguousarray(wblk.reshape(8, 128, 128).transpose(1, 0, 2).reshape(128, 1024))


def _pairswap(w):
    s = w.shape
    return w.reshape(s[0], s[1] // 2, 2)[:, :, ::-1].reshape(s)


def host_prep(inp, depth=DEPTH):
    f = lambda a: np.asarray(a, dtype=np.float32)
    w_in, w_out, w_up, w_down = f(inp["w_in"]), f(inp["w_out"]), f(inp["w_up"]), f(inp["w_down"])
    wu = np.empty((depth * UPL, 128, 1024), np.float32)
    wv = np.empty((depth, 128, 5120), np.float32)
    for l in range(depth):
        W = w_in[l]
        b = l * UPL
        for h in range(4):
            wu[b + U_AQ + h] = _unit(W[:, h * 128:(h + 1) * 128])
            wu[b + U_AK + h] = _unit(W[:, 512 + h * 128:512 + (h + 1) * 128])
            bq = W[:, 1536 + h * 128:1536 + (h + 1) * 128]
            wu[b + U_BQ + h] = _unit(bq)
            wu[b + U_BQS + h] = _unit(_pairswap(bq))
        for g in range(2):
            bk = W[:, 2048 + g * 64:2048 + (g + 1) * 64]
            wu[b + U_BK + g] = _unit(np.concatenate([bk, bk], axis=1))
            bks = _pairswap(bk)
            wu[b + U_BKS + g] = _unit(np.concatenate([bks, bks], axis=1))
        for oc in range(8):
            wu[b + U_WO + oc] = _unit(w_out[l][:, oc * 128:(oc + 1) * 128])
        for fch in range(32):
            wu[b + U_UP + fch] = _unit(w_up[l][:, fch * 128:(fch + 1) * 128])
        for oc in range(8):
            for kg in range(4):
                wu[b + U_DN + oc * 4 + kg] = _unit(w_down[l][kg * 1024:(kg + 1) * 1024, oc * 128:(oc + 1) * 128])
        vcols = np.concatenate([W[:, 1024:1536], W[:, 2176:2304]], axis=1)
        wv[l] = vcols.reshape(8, 128, 640).transpose(1, 0, 2).reshape(128, 5120)
    gains = np.empty((depth, 128, 32), np.float32)
    for l in range(depth):
        for k, nm in enumerate(("g_attn_pre", "g_attn_post", "g_mlp_pre", "g_mlp_post")):
            gains[l, :, k * 8:(k + 1) * 8] = f(inp[nm])[l].reshape(8, 128).T
    sm = np.zeros((depth, 128, 8), np.float32)
    idx = np.arange(128) % 64
    swp = idx ^ 1
    for l in range(depth):
        sm[l, :, 0] = f(inp["g_subln"])[l]
        sm[l, :, 1] = f(inp["g_qnorm"])[l][idx]
        sm[l, :, 2] = f(inp["g_qnorm"])[l][swp]
        sm[l, :, 3] = f(inp["g_knorm"])[l][idx]
        sm[l, :, 4] = f(inp["g_knorm"])[l][swp]
    lam = np.stack([np.concatenate([f(inp[k])[l] for k in ("lambda_q1", "lambda_k1", "lambda_q2", "lambda_k2")])
                    for l in range(depth)])
    cst = np.zeros((3, 128, 128), np.float32)
    cst[0] = np.eye(128)
    cst[1] = np.eye(128)[::-1]
    cst[2] = np.kron(np.eye(2), np.ones((64, 64))) / 64.0
    j = np.arange(BVR_LO, BVR_LO + 2048)
    bk = t5_bucket(L - 1 - j)
    ohr = (bk[None, :] == np.arange(32)[:, None]).astype(np.float32)
    t = np.arange(SEQ)
    row = (t // 64).astype(np.float32)
    col = (t % 64).astype(np.float32)
    inv = (np.float32(10000.0) ** (-np.arange(0, 32, 2, dtype=np.float32) / np.float32(32))).astype(np.float32)
    ang = np.concatenate([row[:, None] * inv[None, :], col[:, None] * inv[None, :]], axis=-1).astype(np.float32)
    ang = np.concatenate([ang, np.zeros((NMETA, 32), np.float32)], axis=0)
    cosT = np.cos(ang).astype(np.float32).T
    sinT = np.sin(ang).astype(np.float32).T
    p = np.arange(128)
    i = (p % 64) // 2
    sign = np.where(p % 2 == 0, -1.0, 1.0).astype(np.float32)
    rope = np.stack([cosT[i], sinT[i] * sign[:, None]]).astype(np.float32)
    return dict(meta=f(inp["meta_tokens"]), wu=wu, wv=wv, gains=gains, sm=sm, lam=lam.astype(np.float32),
                rb=f(inp["rel_bias"]), cst=cst, ohr=ohr, rope=np.ascontiguousarray(rope))


_NC_CACHE = {}


def kernel(**inputs):
    ncores = 8
    xp = np.asarray(inputs["x_prompt"], dtype=np.float32)
    xs = np.asarray(inputs["x_sample"], dtype=np.float32)
    xall = np.concatenate([xp, xs], axis=0)
    nseq = xall.shape[0] // ncores
    shared = host_prep(inputs)
    if nseq not in _NC_CACHE:
        _NC_CACHE[nseq] = build_nc(nseq)
    nc = _NC_CACHE[nseq]
    in_maps = []
    for c in range(ncores):
        m = dict(shared)
        m["x"] = np.ascontiguousarray(xall[c * nseq:(c + 1) * nseq])
        in_maps.append(m)
    res = run_bass_kernel_spmd(nc, in_maps, core_ids=list(range(ncores)))
    yall = np.concatenate([r["y"] for r in res.results], axis=0)
    nb = xp.shape[0]
    return (np.ascontiguousarray(yall[:nb]), np.ascontiguousarray(yall[nb:]))
```
